# Optimizing a Trainium2 kernel written in Bass

```python
import math
import jax, jax.numpy as jnp
from jax import lax
import numpy as np

D_MODEL = 2048
BATCH = 2
SEQ = 16384
DEPTH = 1

D_M = D_MODEL
N_M_HEADS = 8
M_V_DIM = D_M // N_M_HEADS
M_QK_DIM = M_V_DIM // 2
D_QK = N_M_HEADS * M_QK_DIM
D_G = D_MODEL
N_G_HEADS = 8
G_DIM = D_G // N_G_HEADS
D_MIX = D_M + D_G
CHUNK = 128
CONV_W = 4
EPS = 1e-6

COL_SIZES = (D_QK, D_QK, D_M, D_M, D_M, N_M_HEADS, N_M_HEADS, D_G, D_G, D_G)
D_IN = sum(COL_SIZES)

kernel_name = "hymba_mlstm_gmlp_hybrid_layer"


def _rmsnorm(x, w):
    xf = x.astype(jnp.float32)
    r = xf * lax.rsqrt(jnp.mean(xf * xf, axis=-1, keepdims=True) + EPS)
    return (r * w.astype(jnp.float32)).astype(x.dtype)


def _split_cols(h):
    offs = np.cumsum((0,) + COL_SIZES)
    return [h[..., int(offs[i]):int(offs[i + 1])] for i in range(len(COL_SIZES))]


def _causal_depthwise_conv(x, w, b):
    S = x.shape[1]
    xp = jnp.pad(x, ((0, 0), (CONV_W - 1, 0), (0, 0)))
    y = b
    for j in range(CONV_W):
        y = y + w[j] * lax.dynamic_slice_in_dim(xp, j, S, axis=1)
    return y


def _mlstm_chunkwise(q, k, v, log_i, log_f):
    B, S, H, DK = q.shape
    DV = v.shape[-1]
    NC = S // CHUNK

    def to_chunks(a):
        a = a.reshape((B, NC, CHUNK, H) + a.shape[3:])
        perm = (1, 0, 3, 2) + tuple(range(4, a.ndim))
        return a.transpose(perm)

    qc, kc, vc = to_chunks(q), to_chunks(k), to_chunks(v)
    lic, lfc = to_chunks(log_i), to_chunks(log_f)
    causal = jnp.tril(jnp.ones((CHUNK, CHUNK), dtype=bool))

    def step(carry, inp):
        C, n, m = carry
        qb, kb, vb, li, lf = inp
        b = jnp.cumsum(lf, axis=-1)
        D = b[..., :, None] - b[..., None, :] + li[..., None, :]
        D = jnp.where(causal, D, -jnp.inf)
        inter = b + m[..., None]
        m_t = jnp.maximum(inter, jnp.max(D, axis=-1))
        decay = jnp.exp(inter - m_t)
        P = jnp.exp(D - m_t[..., None])
        Sqk = jnp.einsum('bhtd,bhsd->bhts', qb, kb) * P
        num = jnp.einsum('bhts,bhsv->bhtv', Sqk, vb) + decay[..., None] * jnp.einsum('bhtd,bhdv->bhtv', qb, C)
        den = jnp.sum(Sqk, axis=-1) + decay * jnp.einsum('bhtd,bhd->bht', qb, n)
        h = num / jnp.maximum(jnp.abs(den), jnp.exp(-m_t))[..., None]
        bL = b[..., -1]
        wlog = bL[..., None] - b + li
        m_new = jnp.maximum(bL + m, jnp.max(wlog, axis=-1))
        a = jnp.exp(bL + m - m_new)
        ws = jnp.exp(wlog - m_new[..., None])
        C_new = a[..., None, None] * C + jnp.einsum('bhs,bhsd,bhsv->bhdv', ws, kb, vb)
        n_new = a[..., None] * n + jnp.einsum('bhs,bhsd->bhd', ws, kb)
        return (C_new, n_new, m_new), h

    init = (jnp.zeros((B, H, DK, DV), jnp.float32),
            jnp.zeros((B, H, DK), jnp.float32),
            jnp.zeros((B, H), jnp.float32))
    _, hc = lax.scan(step, init, (qc, kc, vc, lic, lfc))
    return hc.transpose(1, 0, 3, 2, 4).reshape(B, S, H, DV)


def _layernorm(x, w, b):
    xf = x.astype(jnp.float32)
    mu = jnp.mean(xf, axis=-1, keepdims=True)
    var = jnp.mean(jnp.square(xf - mu), axis=-1, keepdims=True)
    return (xf - mu) * lax.rsqrt(var + EPS) * w + b


def setup_inputs(seed: int = 0) -> dict:
    key = jax.random.key(seed)
    ks = jax.random.split(key, 14)
    f32 = jnp.float32
    x = jax.random.normal(ks[0], (BATCH, SEQ, D_MODEL), f32)
    norm_w = 1.0 + 0.02 * jax.random.normal(ks[1], (D_MODEL,), f32)
    w_in = jax.random.normal(ks[2], (D_MODEL, D_IN), f32) * D_MODEL ** -0.5
    conv_w = jax.random.normal(ks[3], (CONV_W, 2 * D_QK), f32) * CONV_W ** -0.5
    conv_b = 0.02 * jax.random.normal(ks[4], (2 * D_QK,), f32)
    b_igate = 0.1 * jax.random.normal(ks[5], (N_M_HEADS,), f32)
    b_fgate = 3.0 + 3.0 * jax.random.uniform(ks[6], (N_M_HEADS,), f32)
    mlstm_norm_w = 1.0 + 0.02 * jax.random.normal(ks[7], (D_M,), f32)
    sgu_norm_w = 1.0 + 0.02 * jax.random.normal(ks[8], (D_G,), f32)
    sgu_norm_b = 0.02 * jax.random.normal(ks[9], (D_G,), f32)
    w_spatial = jax.random.normal(ks[10], (N_G_HEADS, CHUNK, CHUNK), f32) * CHUNK ** -0.5
    b_spatial = 1.0 + 0.1 * jax.random.normal(ks[11], (N_G_HEADS, CHUNK), f32)
    w_out = jax.random.normal(ks[12], (D_MIX, D_MODEL), f32) * D_MIX ** -0.5
    final_norm_w = 1.0 + 0.02 * jax.random.normal(ks[13], (D_MODEL,), f32)
    return {"x": x, "norm_w": norm_w, "w_in": w_in, "conv_w": conv_w, "conv_b": conv_b,
            "b_igate": b_igate, "b_fgate": b_fgate, "mlstm_norm_w": mlstm_norm_w,
            "sgu_norm_w": sgu_norm_w, "sgu_norm_b": sgu_norm_b, "w_spatial": w_spatial,
            "b_spatial": b_spatial, "w_out": w_out, "final_norm_w": final_norm_w}


def reference(x, norm_w, w_in, conv_w, conv_b, b_igate, b_fgate, mlstm_norm_w,
              sgu_norm_w, sgu_norm_b, w_spatial, b_spatial, w_out, final_norm_w):
    B, S, _ = x.shape
    NC = S // CHUNK
    f32 = jnp.float32
    h = x
    for _layer in range(DEPTH):
        xn = _rmsnorm(h, norm_w)
        proj = jnp.einsum('bsd,de->bse', xn, w_in).astype(f32)
        q, k, v_m, o_m, z_m, i_pre, f_pre, u_g, v_g, z_g = _split_cols(proj)

        qk = jax.nn.silu(_causal_depthwise_conv(jnp.concatenate([q, k], axis=-1),
                                                 conv_w.astype(f32), conv_b.astype(f32)))
        q = qk[..., :D_QK].reshape(B, S, N_M_HEADS, M_QK_DIM)
        k = qk[..., D_QK:].reshape(B, S, N_M_HEADS, M_QK_DIM) * (M_QK_DIM ** -0.5)
        v = v_m.reshape(B, S, N_M_HEADS, M_V_DIM)
        log_i = i_pre + b_igate.astype(f32)
        log_f = jax.nn.log_sigmoid(f_pre + b_fgate.astype(f32))
        hm = _mlstm_chunkwise(q, k, v, log_i, log_f)
        hm = hm * lax.rsqrt(jnp.mean(hm * hm, axis=-1, keepdims=True) + EPS)
        hm = hm.reshape(B, S, D_M) * mlstm_norm_w.astype(f32)
        y_m = hm * jax.nn.sigmoid(o_m) * jax.nn.silu(z_m)

        u = jax.nn.gelu(u_g)
        vg = jax.nn.gelu(v_g).reshape(B, S, N_G_HEADS, G_DIM)
        vg = _layernorm(vg, sgu_norm_w.reshape(N_G_HEADS, G_DIM).astype(f32),
                        sgu_norm_b.reshape(N_G_HEADS, G_DIM).astype(f32))
        vg = vg.reshape(B, NC, CHUNK, N_G_HEADS, G_DIM)
        w_s = jnp.where(jnp.tril(jnp.ones((CHUNK, CHUNK), dtype=bool)), w_spatial.astype(f32), 0.0)
        sv = jnp.einsum('hts,bnshc->bnthc', w_s, vg) + b_spatial.astype(f32).T[None, None, :, :, None]
        y_g = u * sv.reshape(B, S, D_G) * jax.nn.silu(z_g)

        y = jnp.concatenate([y_m, y_g], axis=-1).astype(x.dtype)
        h = h + jnp.einsum('bse,ed->bsd', y, w_out)
    return _rmsnorm(h, final_norm_w)
```

```python
import contextlib
import numpy as np
import concourse.bass as bass
import concourse.mybir as mybir
from concourse.bass_utils import run_bass_kernel_spmd

F32 = mybir.dt.float32
BF16 = mybir.dt.bfloat16
AF = mybir.ActivationFunctionType
ALU = mybir.AluOpType
AX = mybir.AxisListType

D = 2048
NH = 8
DQK = 1024
DIN = 14352
DMIX = 4096
KC = 16
OFF_Q, OFF_K, OFF_V, OFF_O, OFF_Z, OFF_I, OFF_F, OFF_U, OFF_VG, OFF_ZG = (
    0, 1024, 2048, 4096, 6144, 8192, 8200, 8208, 10256, 12304)
EPS = 1e-6
NCORES = 8
NSEG = 4
NTOK = 16384 // NSEG
NCH = NTOK // 128
TT = NTOK // 512
VW = 258
NCHX = 16384 // 128
TTX = 16384 // 512
PRE_T = TTX - TT
PRE_C = NCHX - NCH
USE_CC = False
SAME_ENGINE_SYNC = True


class Prog:
    def __init__(self, nc, same_engine_sync=True):
        self.nc = nc
        self.ops = []
        self.cnt = {e: 0 for e in ("pe", "act", "dve", "pool", "sp")}
        self.res = {}
        self.waited = {}
        self.dma_cnt = {}
        self.same_engine_sync = same_engine_sync
        self.sem_keys = []

    def _semkey(self, k):
        if k not in self.sem_keys:
            self.sem_keys.append(k)
        return k

    def _deps(self, reads, writes):
        deps = []
        for k in reads:
            r = self.res.get(k)
            if r and r["w"] is not None:
                deps.append(r["w"])
        for k in writes:
            r = self.res.get(k)
            if r:
                if r["w"] is not None:
                    deps.append(r["w"])
                deps.extend(r["r"])
        return deps

    def _commit(self, ev, reads, writes):
        for k in reads:
            r = self.res.setdefault(k, {"w": None, "r": []})
            r["r"].append(ev)
        for k in writes:
            self.res[k] = {"w": ev, "r": []}

    def op(self, engine, fn, reads=(), writes=(), dma=None):
        deps = self._deps(reads, writes)
        waits = {}
        for (sk, val, eng_of_ev) in deps:
            if eng_of_ev == engine:
                if engine == "pe" or not self.same_engine_sync:
                    continue
            if self.waited.get((engine, sk), 0) >= val:
                continue
            waits[sk] = max(waits.get(sk, 0), val)
        for sk, val in waits.items():
            self.waited[(engine, sk)] = val
        if dma is not None:
            sk = self._semkey(("dma", dma))
            self.dma_cnt[sk] = self.dma_cnt.get(sk, 0) + 16
            ev = (sk, self.dma_cnt[sk], None)
            inc = (sk, 16)
        else:
            sk = self._semkey(("eng", engine))
            self.cnt[engine] += 1
            ev = (sk, self.cnt[engine], engine)
            inc = (sk, 1)
        self._commit(ev, reads, writes)
        self.ops.append((engine, fn, sorted(waits.items(), key=lambda x: str(x[0])), inc))
        return ev

    def barrier(self):
        allw = {}
        for k in self.sem_keys:
            v = self.cnt[k[1]] if k[0] == "eng" else self.dma_cnt.get(k, 0)
            if v > 0:
                allw[k] = v
        for e in ("sp", "act", "dve", "pool", "pe"):
            w = {}
            for k, v in allw.items():
                if k == ("eng", e):
                    continue
                if self.waited.get((e, k), 0) >= v:
                    continue
                w[k] = v
                self.waited[(e, k)] = v
            if w:
                self.ops.append((e, None, sorted(w.items(), key=lambda x: str(x[0])), None))
        self.res = {}

    def final_wait(self, engine, keys):
        deps = self._deps(keys, ())
        waits = {}
        for (sk, val, _e) in deps:
            waits[sk] = max(waits.get(sk, 0), val)
        self.ops.append((engine, None, sorted(waits.items(), key=lambda x: str(x[0])), None))

    def emit(self):
        nc = self.nc
        with contextlib.ExitStack() as st:
            sems = {}
            for i, k in enumerate(self.sem_keys):
                sems[k] = st.enter_context(nc.semaphore("s%d" % i))
            block = st.enter_context(nc.Block())
            ops = self.ops

            def run(engname, eng):
                for (e, fn, waits, inc) in ops:
                    if e != engname:
                        continue
                    for sk, val in waits:
                        eng.wait_ge(sems[sk], val)
                    if fn is None:
                        continue
                    ins = fn(eng)
                    if inc is not None:
                        ins.then_inc(sems[inc[0]], inc[1])

            @block.sync
            def _(eng):
                run("sp", eng)

            @block.scalar
            def _(eng):
                run("act", eng)

            @block.vector
            def _(eng):
                run("dve", eng)

            @block.gpsimd
            def _(eng):
                run("pool", eng)

            @block.tensor
            def _(eng):
                run("pe", eng)


class Arena:
    def __init__(self, ap, words):
        self.ap, self.words, self.off = ap, words, 0

    def mark(self):
        return self.off

    def reset(self, m):
        self.off = m

    def f32(self, n):
        a = self.ap[:, self.off:self.off + n]
        self.off += n
        assert self.off <= self.words, ("arena overflow", self.off, self.words)
        return a

    def bf16(self, n):
        w = (n + 1) // 2
        a = self.ap[:, self.off:self.off + w].bitcast(BF16)
        self.off += w
        assert self.off <= self.words, ("arena overflow", self.off, self.words)
        return a[:, 0:n]


def build_program():
    nc = bass.Bass("TRN2", target_bir_lowering=False)
    dt_in = lambda name, shape: nc.dram_tensor(name, shape, F32, kind="ExternalInput").ap()
    x_d = dt_in("x", [NTOK, D])
    xh_d = dt_in("xh", [128, D])
    nw_d = dt_in("norm_w", [D])
    win_d = dt_in("w_in", [D, DIN])
    cw_d = dt_in("conv_wT", [128, 64])
    cb_d = dt_in("conv_bT", [128, 16])
    bif_d = dt_in("b_if", [16])
    mnw_d = dt_in("mlstm_norm_w", [D])
    sgw_d = dt_in("sgu_norm_w", [D])
    sgb_d = dt_in("sgu_norm_b", [D])
    wsp_d = dt_in("w_spT", [NH, 128, 128])
    bsp_d = dt_in("b_spatial", [NH, 128])
    wout_d = dt_in("w_out", [DMIX, D])
    fnw_d = dt_in("final_norm_w", [D])
    idn_d = dt_in("ident", [128, 128])
    tri_d = dt_in("tri", [128, 128])
    ones_d = dt_in("ones", [128, 128])
    xe_d = dt_in("xe", [PRE_C * 128, D])
    mch_d = dt_in("mch", [NCHX])
    out_d = nc.dram_tensor("out", [NTOK, D], F32, kind="ExternalOutput").ap()
    xT_d = nc.dram_tensor("xT_s", [128, KC, (NCHX + 1) * 128], BF16).ap()
    kT_d = nc.dram_tensor("kT_s", [NH, 128, NTOK], BF16).ap()
    kt_d = nc.dram_tensor("kt_s", [NH, NCH, 128, 128], BF16).ap()
    v_d = nc.dram_tensor("v_s", [NH, NCH, 128, VW], BF16).ap()
    yT_d = nc.dram_tensor("yT_s", [NCH, 128, 32, 128], BF16).ap()

    win_v = win_d.rearrange("(kc p) c -> p kc c", p=128)
    wout_v = wout_d.rearrange("(fc p) c -> p fc c", p=128)

    with contextlib.ExitStack() as st:
        ARENA_WORDS = 51200
        arena_t = st.enter_context(nc.sbuf_tensor("arena", [128, ARENA_WORDS], F32))
        A = Arena(arena_t[:, :], ARENA_WORDS)
        banks = [st.enter_context(nc.psum_tensor("bank%d" % i, [128, 512], F32)) for i in range(8)]
        PS = [b[:, :] for b in banks]
        PSB = [b[:, :].bitcast(BF16) for b in banks]
        P = Prog(nc, same_engine_sync=SAME_ENGINE_SYNC)

        def MM(out, lhsT, rhs, start, stop, R, W):
            P.op("pe", lambda e: e.matmul(out, lhsT=lhsT, rhs=rhs, start=start, stop=stop), R, W)

        def ACT(out, in_, func, R, W, **kw):
            P.op("act", lambda e: e.activation(out=out, in_=in_, func=func, **kw), R, W)

        def DMA(q, out, in_, R, W, key):
            P.op(q, lambda e: e.dma_start(out=out, in_=in_), R, W, dma=key)

        def TS(out, in0, s1, s2, op0, op1, R, W, eng="dve"):
            if op1 is None:
                P.op(eng, lambda e: e.tensor_scalar(out, in0, s1, None, op0), R, W)
            else:
                P.op(eng, lambda e: e.tensor_scalar(out, in0, s1, s2, op0, op1), R, W)

        def STT(out, in0, scalar, in1, op0, op1, R, W, eng="dve"):
            P.op(eng, lambda e: e.scalar_tensor_tensor(out=out, in0=in0, scalar=scalar, in1=in1, op0=op0, op1=op1), R, W)

        def TT_(out, in0, in1, op, R, W, eng="dve"):
            P.op(eng, lambda e: e.tensor_tensor(out, in0, in1, op), R, W)

        def CP(out, in_, R, W, eng="dve"):
            if eng == "act":
                P.op("act", lambda e: e.copy(out, in_), R, W)
            else:
                P.op(eng, lambda e: e.tensor_copy(out, in_), R, W)

        def RSQRT(out, in_, R, W):
            n_ = out.shape[-1]
            P.op("pool", lambda e: e.tensor_tensor(out, in_, mhalf[:, 0:n_], ALU.pow), R + ["mhalf"], W)

        def RECIP(out, in_, R, W):
            P.op("dve", lambda e: e.reciprocal(out, in_), R, W)

        identf = A.f32(128)
        trif = A.f32(128)
        onesf = A.f32(128)
        ident = A.bf16(128)
        cw = A.f32(64).rearrange("p (c j) -> p c j", j=4)
        cb = A.f32(16)
        bif = A.f32(16)
        S = A.f32(NH * VW).rearrange("p (h v) -> p h v", v=VW)
        Sinit = A.f32(NH * VW).rearrange("p (h v) -> p h v", v=VW)
        Sbf = A.bf16(2 * VW).rearrange("p (s v) -> p s v", v=VW)
        small = A.f32(64)
        mhalf = A.f32(16)
        P.op("pool", lambda e: e.memset(mhalf, -0.5), [], ["mhalf"])
        pm_p4 = A.mark()
        gpre = A.f32(NCHX * 16).rearrange("p (c g) -> p c g", g=16)
        NG = NCHX * 8
        li = A.f32(NG)
        spl = A.f32(NG)
        cs = A.f32(NG)
        gg = A.f32(NG)
        ee = A.f32(NG)
        aL = A.f32(NG)
        mch = A.f32(NCHX)
        v3 = lambda a: a.rearrange("p (c h) -> p c h", h=8)
        DMA("sp", identf, idn_d, [], ["identf"], "c0")
        DMA("sp", trif, tri_d, [], ["trif"], "c1")
        DMA("sp", onesf, ones_d, [], ["onesf"], "c2")
        DMA("sp", cw.rearrange("p c j -> p (c j)"), cw_d, [], ["cw"], "c3")
        DMA("sp", cb, cb_d, [], ["cb"], "c4")
        DMA("sp", bif, bif_d.partition_broadcast(128), [], ["bif"], "c5")
        DMA("sp", mch, mch_d.partition_broadcast(128), [], ["mch"], "c7")
        CP(ident, identf, ["identf"], ["ident"])
        pm_persist = A.mark()

        nw = A.f32(D)
        wif = A.bf16(KC * 16).rearrange("p (k c) -> p k c", c=16)
        NB = 8
        GP = 4
        xt = [A.f32(D) for _ in range(NB)]
        xn = [A.bf16(D) for _ in range(GP)]
        junk = A.bf16(D)
        xTs = [A.bf16(KC * 512).rearrange("p (k t) -> p k t", t=512) for _ in range(2)]
        st0 = A.f32(32)
        DMA("sp", nw, nw_d.partition_broadcast(128), [], ["nw"], "c6")
        DMA("pool", wif, win_v[:, :, OFF_I:OFF_I + 16], [], ["wif"], "wif")
        groups0 = [list(range(g, min(g + GP, NCHX + 1))) for g in range(0, NCHX + 1, GP)]

        def p0_stageA(gi):
            cs_ = groups0[gi]
            p = gi % 2
            base = p * 16
            n = len(cs_)
            for k, c in enumerate(cs_):
                s = c % NB
                src = xh_d if c == 0 else (xe_d[(c - 1) * 128:c * 128, :] if c <= PRE_C
                                           else x_d[(c - 1 - PRE_C) * 128:(c - PRE_C) * 128, :])
                DMA("sp", xt[s], src, [], ["xt%d" % s], "xt%d" % s)
                ACT(junk, xt[s], AF.Square, ["xt%d" % s], ["ssq%d_%d" % (p, k)], accum_out=st0[:, base + k:base + k + 1])
            TS(st0[:, base + 4:base + 4 + n], st0[:, base:base + n], 1.0 / D, EPS, ALU.mult, ALU.add,
               ["ssq%d_%d" % (p, k) for k in range(n)], ["sv%d" % p])
            RSQRT(st0[:, base + 12:base + 12 + n], st0[:, base + 4:base + 4 + n], ["sv%d" % p], ["rstd%d" % p])

        def p0_stageB(gi):
            cs_ = groups0[gi]
            p = gi % 2
            base = p * 16
            n = len(cs_)
            xs = xTs[p]
            for k, c in enumerate(cs_):
                s = c % NB
                STT(xn[k], xt[s], st0[:, base + 12 + k:base + 13 + k], nw, ALU.mult, ALU.mult,
                    ["xt%d" % s, "rstd%d" % p, "nw"], ["xn%d" % k])
            for k, c in enumerate(cs_):
                for half in range(2):
                    b = (2 * c + half) % 6
                    for j in range(8):
                        kc = half * 8 + j
                        P.op("pe", lambda e, o=PSB[b][:, j * 128:(j + 1) * 128], i=xn[k][:, kc * 128:(kc + 1) * 128]:
                             e.transpose(o, i, ident), ["xn%d" % k, "ident"], ["ps%d" % b])
                    CP(xs[:, half * 8:(half + 1) * 8, k * 128:(k + 1) * 128], PSB[b].rearrange("p (k t) -> p k t", t=128),
                       ["ps%d" % b], ["xTs%d_%d_%d" % (p, k, half)], eng=("act" if half == 0 else "dve"))
            allk = ["xTs%d_%d_%d" % (p, k, half) for k in range(n) for half in range(2)]
            c0 = cs_[0]
            DMA("sp", xT_d[:, :, c0 * 128:(c0 + n) * 128], xs[:, :, 0:n * 128], allk, [], "xTst%d" % p)
            for k, c in enumerate(cs_):
                if c >= 1:
                    bg = 6 + (c % 2)
                    for kc in range(KC):
                        MM(PS[bg][:, 0:16], xs[:, kc, k * 128:(k + 1) * 128], wif[:, kc, :], kc == 0, kc == KC - 1,
                           ["xTs%d_%d_0" % (p, k), "xTs%d_%d_1" % (p, k), "wif"], ["ps%d" % bg])
                    CP(gpre[:, c - 1, :], PS[bg][:, 0:16], ["ps%d" % bg], ["gpre%d" % (c % 2)], eng="act")

        p0_stageA(0)
        for gi in range(len(groups0)):
            if gi + 1 < len(groups0):
                p0_stageA(gi + 1)
            p0_stageB(gi)
        for c in range(NCHX):
            TT_(v3(li)[:, c, :], gpre[:, c, 0:8], bif[:, 0:8], ALU.add, ["gpre0", "gpre1", "bif"], ["li"])
            TT_(v3(spl)[:, c, :], gpre[:, c, 8:16], bif[:, 8:16], ALU.add, ["gpre0", "gpre1", "bif"], ["spl"])
        ACT(spl, spl, AF.Exp, ["spl"], ["spl"], scale=-1.0)
        ACT(spl, spl, AF.Ln, ["spl"], ["spl"], bias=1.0)
        for o in range(0, NG, 512):
            n = min(512, NG - o)
            MM(PS[6][:, 0:n], trif, spl[:, o:o + n], True, True, ["trif", "spl"], ["ps6"])
            CP(cs[:, o:o + n], PS[6][:, 0:n], ["ps6"], ["cs"])
            MM(PS[7][:, 0:n], onesf, spl[:, o:o + n], True, True, ["onesf", "spl"], ["ps7"])
            ACT(aL[:, o:o + n], PS[7][:, 0:n], AF.Exp, ["ps7"], ["aL"], scale=-1.0)
        ACT(ee, cs, AF.Exp, ["cs"], ["ee"], scale=-1.0)
        TT_(gg, li, cs, ALU.add, ["li", "cs"], ["gg"])
        P.op("dve", lambda e: e.memset(small[:, 8:9], -0.5 * float(np.log(128.0))), [], ["lnsc"])
        ACT(gg, gg, AF.Exp, ["gg", "lnsc"], ["gg"], bias=small[:, 8:9])
        TT_(v3(gg), v3(gg), mch.unsqueeze(2).to_broadcast([128, NCHX, 8]), ALU.mult, ["gg", "mch"], ["gg"])
        gA = li
        TT_(gA, gg, aL, ALU.mult, ["gg", "aL", "li"], ["gA"])
        P.barrier()
        A.reset(pm_persist)

        def load_xT(buf, slot, i):
            t0 = 128 + 512 * i
            DMA("sp", buf, xT_d[:, :, t0 - 3:t0 + 512], [], ["xTt%d" % slot], "xTt%d" % slot)

        def feat_proj_conv(W, xTt, slot, fcidx, pre, acc, outT, bmain, bhalo, wkeys, outkey):
            fpc_mm(W, xTt, slot, pre, bmain, PS[bhalo][:, 0:3], "ps%d" % bhalo, wkeys)
            fpc_conv(fcidx, pre, acc, outT, outkey)

        def fpc_mm(W, xTt, slot, pre, bmain, halo_ap, halo_key, wkeys):
            for kc in range(KC):
                MM(PS[bmain], W[:, kc, :], xTt[:, kc, 3:515], kc == 0, kc == KC - 1,
                   wkeys + ["xTt%d" % slot], ["ps%d" % bmain])
            for kc in range(KC):
                MM(halo_ap, W[:, kc, :], xTt[:, kc, 0:3], kc == 0, kc == KC - 1,
                   wkeys + ["xTt%d" % slot], [halo_key])
            CP(pre[:, 3:515], PS[bmain], ["ps%d" % bmain], ["pre_m"], eng="act")
            CP(pre[:, 0:3], halo_ap, [halo_key], ["pre_h"], eng="act")

        def fpc_conv(fcidx, pre, acc, outT, outkey):
            TS(acc, pre[:, 3:515], cw[:, fcidx, 3:4], cb[:, fcidx:fcidx + 1], ALU.mult, ALU.add,
               ["pre_m", "cw", "cb"], ["acc"])
            for j in range(3):
                STT(acc, pre[:, j:j + 512], cw[:, fcidx, j:j + 1], acc, ALU.mult, ALU.add,
                    ["pre_m", "pre_h", "cw", "acc"], ["acc"])
            ACT(outT, acc, AF.Silu, ["acc"], [outkey])

        def phase1():
            WkA = A.bf16(KC * 1024).rearrange("p (k c) -> p k c", c=1024)
            WvA = A.bf16(KC * 2048).rearrange("p (k c) -> p k c", c=2048)
            Wk = [WkA[:, :, h_ * 128:(h_ + 1) * 128] for h_ in range(NH)]
            Wv = [WvA[:, :, h_ * 256:(h_ + 1) * 256] for h_ in range(NH)]
            xTt = [A.bf16(KC * 515).rearrange("p (k t) -> p k t", t=515) for _ in range(2)]
            pre = A.f32(515)
            acc = A.f32(512)
            kT = [A.bf16(512) for _ in range(2)]
            ktok = [A.bf16(512).rearrange("p (j d) -> p j d", d=128) for _ in range(2)]
            vp = [A.bf16(4 * VW).rearrange("p (j v) -> p j v", v=VW) for _ in range(2)]
            P.op("dve", lambda e: e.memset(S.rearrange("p h v -> p (h v)"), 0.0), [], ["S"])

            def loadW(h):
                DMA("pool", Wk[h], win_v[:, :, OFF_K + h * 128:OFF_K + (h + 1) * 128], [], ["Wk%d" % h], "Wk%d" % h)
                DMA("pool", Wv[h], win_v[:, :, OFF_V + h * 256:OFF_V + (h + 1) * 256], [], ["Wv%d" % h], "Wv%d" % h)

            def K_mm(h, i, s, ws, xs_):
                fpc_mm(Wk[ws], xTt[xs_], xs_, pre, 0, PS[4][:, 300:303], "ps4", ["Wk%d" % ws])

            def K_conv(h, i, s, ws, xs_):
                fpc_conv(8 + h, pre, acc, kT[s], "kT%d" % s)
                if i >= PRE_T:
                    io = i - PRE_T
                    DMA("sp", kT_d[h, :, io * 512:(io + 1) * 512], kT[s], ["kT%d" % s], [], "kTst%d" % s)

            def V_part(h, i, s, ws, xs_, js):
                for j in js:
                    c = i * 4 + j
                    b = 2 + (j % 2)
                    for kc in range(KC):
                        MM(PS[b][:, 0:256], xTt[xs_][:, kc, 3 + j * 128:3 + (j + 1) * 128], Wv[ws][:, kc, :],
                           kc == 0, kc == KC - 1, ["xTt%d" % xs_, "Wv%d" % ws], ["ps%d" % b])
                    sc_ = (v3(gg) if i >= PRE_T else v3(gA))[:, c, h:h + 1]
                    ACT(vp[s][:, j, 0:256], PS[b][:, 0:256], AF.Identity, ["ps%d" % b, "gg", "gA"], ["vp%d_%d" % (s, j)],
                        scale=sc_)
                    ACT(vp[s][:, j, 256:257], onesf[:, 0:1], AF.Identity, ["onesf", "gg", "gA", "vp%d_%d" % (s, j)],
                        ["vp%d_%d" % (s, j)], scale=sc_)

            def T_part(h, i, s):
                for j in range(4):
                    P.op("pe", lambda e, o=PSB[4][:, j * 128:(j + 1) * 128], a=kT[s][:, j * 128:(j + 1) * 128]:
                         e.transpose(o, a, ident), ["kT%d" % s, "ident"], ["ps4"])
                CP(ktok[s].rearrange("p j d -> p (j d)"), PSB[4][:, 0:512], ["ps4"], ["ktok%d" % s], eng="act")

            def S_part(h, i, s):
                if i < PRE_T:
                    for j in range(4):
                        c = i * 4 + j
                        bs = (1, 5, 6, 7)[j]
                        MM(PS[bs][:, 0:257], ktok[s][:, j, :], vp[s][:, j, 0:257], True, True,
                           ["ktok%d" % s, "vp%d_%d" % (s, j)], ["ps%d" % bs])
                        STT(S[:, h, 0:257], S[:, h, 0:257], v3(aL)[:, c, h:h + 1], PS[bs][:, 0:257], ALU.mult, ALU.add,
                            ["ps%d" % bs, "aL", "S"], ["S"])
                    if i == PRE_T - 1:
                        CP(Sinit[:, h, :], S[:, h, :], ["S"], ["Sinit"])
                else:
                    io = i - PRE_T
                    DMA("sp", kt_d[h, io * 4:(io + 1) * 4, :, :].rearrange("c p d -> p c d"), ktok[s],
                        ["ktok%d" % s], [], "ktst%d" % s)
                    DMA("sp", v_d[h, io * 4:(io + 1) * 4, :, :].rearrange("c p v -> p c v"), vp[s],
                        ["vp%d_%d" % (s, j) for j in range(4)], [], "vst%d" % s)

            load_xT(xTt[0], 0, 0)
            for h in range(NH):
                loadW(h)
            it = 0
            prev = None
            for i in range(TTX):
                xs_ = i % 2
                if i + 1 < TTX:
                    load_xT(xTt[(i + 1) % 2], (i + 1) % 2, i + 1)
                for h in range(NH):
                    s = it % 2
                    if prev is not None:
                        T_part(*prev)
                    K_mm(h, i, s, h, xs_)
                    V_part(h, i, s, h, xs_, (0, 1))
                    K_conv(h, i, s, h, xs_)
                    V_part(h, i, s, h, xs_, (2, 3))
                    if prev is not None:
                        S_part(*prev)
                    prev = (h, i, s)
                    it += 1
            T_part(*prev)
            S_part(*prev)

        phase1()
        P.barrier()
        A.reset(pm_persist)
        pm2 = A.mark()

        def phase2():
            W = [A.bf16(KC * 768).rearrange("p (k c) -> p k c", c=768) for _ in range(2)]
            xTt = [A.bf16(KC * 515).rearrange("p (k t) -> p k t", t=515) for _ in range(2)]
            wsf = A.f32(128)
            wsb = [A.bf16(128) for _ in range(2)]
            bsph = [A.f32(512) for _ in range(2)]
            sgw = [A.f32(256) for _ in range(2)]
            sgb = [A.f32(256) for _ in range(2)]
            usb2 = [A.f32(512) for _ in range(2)]
            sq2 = [A.f32(512) for _ in range(2)]
            th2 = [A.f32(512) for _ in range(2)]
            gctr = [0]
            gu = [A.f32(512) for _ in range(2)]
            sz = [A.f32(512) for _ in range(2)]
            gv = A.f32(1024).rearrange("p (j c) -> p j c", c=256)
            sqv = A.f32(1024).rearrange("p (j c) -> p j c", c=256)
            gvn = A.f32(1024).rearrange("p (j c) -> p j c", c=256)
            vgn = A.bf16(1024).rearrange("p (j c) -> p j c", c=256)
            st4 = A.f32(32)
            tmp = A.f32(512)
            yT = [A.bf16(512) for _ in range(2)]

            def loadW(h):
                s = h % 2
                for g, off in enumerate((OFF_U, OFF_VG, OFF_ZG)):
                    DMA("pool", W[s][:, :, g * 256:(g + 1) * 256], win_v[:, :, off + h * 256:off + (h + 1) * 256],
                        [], ["W%d" % s], "W2_%d" % s)
                DMA("sp", wsf, wsp_d[h], [], ["wsf"], "wsf")
                TT_(wsb[s], wsf, trif, ALU.mult, ["wsf", "trif"], ["wsb%d" % s])
                for r in range(4):
                    DMA("sp", bsph[s][:, r * 128:(r + 1) * 128], bsp_d[h].partition_broadcast(128), [], ["bsph%d" % s], "bsp%d" % s)
                TS(bsph[s], bsph[s], 0.5, None, ALU.mult, None, ["bsph%d" % s], ["bsph%d" % s])
                DMA("sp", sgw[s], sgw_d[h * 256:(h + 1) * 256].partition_broadcast(128), [], ["sgw%d" % s], "sgw%d" % s)
                DMA("sp", sgb[s], sgb_d[h * 256:(h + 1) * 256].partition_broadcast(128), [], ["sgb%d" % s], "sgb%d" % s)

            def gelu_s1(ps_ap, pskey, n):
                q = gctr[0] % 2
                gctr[0] += 1
                usb, sq = usb2[q], sq2[q]
                ku, ks = "usb%d" % q, "sq%d" % q
                CP(usb[:, 0:n], ps_ap, [pskey], [ku], eng="act")
                ACT(sq[:, 0:n], ps_ap, AF.Square, [pskey], [ks])
                TS(sq[:, 0:n], sq[:, 0:n], 0.044715, 1.0, ALU.mult, ALU.add, [ks], [ks])
                TT_(sq[:, 0:n], sq[:, 0:n], usb[:, 0:n], ALU.mult, [ks, ku], [ks])
                return q

            def gelu_s2(q, dst, dstkey, n):
                usb, sq, th = usb2[q], sq2[q], th2[q]
                ku, ks, kt = "usb%d" % q, "sq%d" % q, "th%d" % q
                ACT(th[:, 0:n], sq[:, 0:n], AF.Tanh, [ks], [kt], scale=0.7978845608028654)
                STT(dst, th[:, 0:n], 1.0, usb[:, 0:n], ALU.add, ALU.mult, [kt, ku], [dstkey])

            loadW(0)
            it = 0
            load_xT(xTt[0], 0, PRE_T)
            for h in range(NH):
                ws = h % 2
                wk = ["W%d" % ws]
                if h + 1 < NH:
                    loadW(h + 1)
                for i in range(TT):
                    s = it % 2
                    nxt = it + 1
                    if nxt < NH * TT:
                        load_xT(xTt[nxt % 2], nxt % 2, PRE_T + nxt % TT)
                    xk = ["xTt%d" % s]
                    qs = []
                    for half in range(2):
                        b = half
                        for jj in range(2):
                            j = half * 2 + jj
                            for kc in range(KC):
                                MM(PS[b][:, jj * 256:(jj + 1) * 256], xTt[s][:, kc, 3 + j * 128:3 + (j + 1) * 128],
                                   W[ws][:, kc, 256:512], kc == 0, kc == KC - 1, wk + xk, ["ps%d" % b])
                        qs.append(gelu_s1(PS[b], "ps%d" % b, 512))
                    for half in range(2):
                        gelu_s2(qs[half], gv[:, half * 2:half * 2 + 2, :].rearrange("p j c -> p (j c)"), "gv%d" % half, 512)
                    gvk = ["gv0", "gv1"]
                    P.op("dve", lambda e: e.tensor_reduce(out=st4[:, 0:4], in_=gv, axis=AX.X, op=ALU.add), gvk, ["st_s1"])
                    ACT(sqv.rearrange("p j c -> p (j c)"), gv.rearrange("p j c -> p (j c)"), AF.Square, gvk, ["sqv"])
                    P.op("dve", lambda e: e.tensor_reduce(out=st4[:, 4:8], in_=sqv, axis=AX.X, op=ALU.add), ["sqv"], ["st_s2"])
                    TS(st4[:, 8:12], st4[:, 0:4], 1.0 / 256, None, ALU.mult, None, ["st_s1"], ["st_mean"])
                    TT_(st4[:, 12:16], st4[:, 8:12], st4[:, 8:12], ALU.mult, ["st_mean"], ["st_msq"])
                    STT(st4[:, 16:20], st4[:, 4:8], 1.0 / 256, st4[:, 12:16], ALU.mult, ALU.subtract, ["st_s2", "st_msq"], ["st_var"])
                    TS(st4[:, 16:20], st4[:, 16:20], 4.0 * EPS, None, ALU.add, None, ["st_var"], ["st_var"])
                    RSQRT(st4[:, 24:28], st4[:, 16:20], ["st_var"], ["st_rstd"])

                    def uz_block(cb_):
                        b = 2 + cb_
                        for kc in range(KC):
                            MM(PS[b], W[ws][:, kc, cb_ * 128:(cb_ + 1) * 128], xTt[s][:, kc, 3:515],
                               kc == 0, kc == KC - 1, wk + xk, ["ps%d" % b])
                        q_ = gelu_s1(PS[b], "ps%d" % b, 512)
                        b2 = 4 + cb_
                        for kc in range(KC):
                            MM(PS[b2], W[ws][:, kc, 512 + cb_ * 128:512 + (cb_ + 1) * 128], xTt[s][:, kc, 3:515],
                               kc == 0, kc == KC - 1, wk + xk, ["ps%d" % b2])
                        ACT(sz[cb_], PS[b2], AF.Silu, ["ps%d" % b2], ["sz%d" % cb_])
                        return q_

                    qu0 = uz_block(0)
                    STT(st4[:, 28:32], st4[:, 8:12], -1.0, st4[:, 24:28], ALU.mult, ALU.mult, ["st_mean", "st_rstd"], ["st_nmr"])
                    for j in range(4):
                        TS(gvn[:, j, :], gv[:, j, :], st4[:, 24 + j:25 + j], st4[:, 28 + j:29 + j], ALU.mult, ALU.add,
                           ["gv%d" % (j // 2), "st_rstd", "st_nmr"], ["gvn%d" % j])
                        TT_(gvn[:, j, :], gvn[:, j, :], sgw[ws], ALU.mult, ["gvn%d" % j, "sgw%d" % ws], ["gvn%d" % j])
                        TT_(vgn[:, j, :], gvn[:, j, :], sgb[ws], ALU.add, ["gvn%d" % j, "sgb%d" % ws], ["vgn%d" % j])
                    qu1 = uz_block(1)
                    gelu_s2(qu0, gu[0], "gu0", 512)
                    gelu_s2(qu1, gu[1], "gu1", 512)
                    for cb_ in range(2):
                        b = 6 + cb_
                        for j in range(4):
                            MM(PS[b][:, j * 128:(j + 1) * 128], vgn[:, j, cb_ * 128:(cb_ + 1) * 128], wsb[ws], True, True,
                               ["vgn%d" % j, "wsb%d" % ws], ["ps%d" % b])
                        STT(tmp, PS[b], 0.5, bsph[ws], ALU.mult, ALU.add, ["ps%d" % b, "bsph%d" % ws], ["tmp"])
                        TT_(tmp, tmp, gu[cb_], ALU.mult, ["tmp", "gu%d" % cb_], ["tmp"])
                        TT_(yT[cb_], tmp, sz[cb_], ALU.mult, ["tmp", "sz%d" % cb_], ["yT%d" % cb_])
                        fc = 16 + 2 * h + cb_
                        DMA("sp", yT_d[i * 4:(i + 1) * 4, :, fc, :].rearrange("c p t -> p c t"),
                            yT[cb_].rearrange("p (c t) -> p c t", t=128), ["yT%d" % cb_], [], "yTst%d" % cb_)
                    it += 1

        phase2()
        P.barrier()
        A.reset(pm2)

        def phase3():
            Wq = [A.bf16(KC * 128).rearrange("p (k c) -> p k c", c=128) for _ in range(2)]
            Woz = [A.bf16(KC * 512).rearrange("p (k c) -> p k c", c=512) for _ in range(2)]
            mnw = [A.f32(256) for _ in range(2)]
            xTt = [A.bf16(KC * 515).rearrange("p (k t) -> p k t", t=515) for _ in range(2)]
            pre = A.f32(515)
            acc = A.f32(512)
            qT = A.bf16(512)
            kT = [A.bf16(512) for _ in range(2)]
            ktok = [A.bf16(512).rearrange("p (j d) -> p j d", d=128) for _ in range(2)]
            vp = [A.bf16(4 * VW).rearrange("p (j v) -> p j v", v=VW) for _ in range(2)]
            ATm = A.bf16(512).rearrange("p (j t) -> p j t", t=128)
            to = [A.f32(256) for _ in range(2)]
            szz = [A.f32(256) for _ in range(2)]
            y2 = [[A.f32(256) for _ in range(4)] for _ in range(2)]
            y1 = A.f32(256)
            ybf = [A.bf16(256) for _ in range(2)]
            jk = A.bf16(256)
            stt = [A.f32(64) for _ in range(2)]
            yT = A.bf16(1024)
            CP(S.rearrange("p h v -> p (h v)"), Sinit.rearrange("p h v -> p (h v)"), ["Sinit"], ["S"])

            def loadW(h):
                s = h % 2
                DMA("pool", Wq[s], win_v[:, :, OFF_Q + h * 128:OFF_Q + (h + 1) * 128], [], ["Wq%d" % s], "Wq%d" % s)
                DMA("pool", Woz[s][:, :, 0:256], win_v[:, :, OFF_O + h * 256:OFF_O + (h + 1) * 256], [], ["Woz%d" % s], "Woz%d" % s)
                DMA("pool", Woz[s][:, :, 256:512], win_v[:, :, OFF_Z + h * 256:OFF_Z + (h + 1) * 256], [], ["Woz%d" % s], "Woz%d" % s)
                DMA("sp", mnw[s], mnw_d[h * 256:(h + 1) * 256].partition_broadcast(128), [], ["mnw%d" % s], "mnw%d" % s)

            def load_tile(itn):
                h_, i_ = divmod(itn, TT)
                s_ = itn % 2
                load_xT(xTt[s_], s_, PRE_T + i_)
                DMA("sp", kT[s_], kT_d[h_, :, i_ * 512:(i_ + 1) * 512], [], ["kT%d" % s_], "kTl%d" % s_)
                DMA("sp", ktok[s_], kt_d[h_, i_ * 4:(i_ + 1) * 4, :, :].rearrange("c p d -> p c d"), [], ["ktok%d" % s_], "ktl%d" % s_)
                DMA("sp", vp[s_], v_d[h_, i_ * 4:(i_ + 1) * 4, :, :].rearrange("c p v -> p c v"), [], ["vp%d" % s_], "vl%d" % s_)

            def QA_part(h, i, s, ws):
                feat_proj_conv(Wq[ws], xTt[s], s, h, pre, acc, qT, 0, 3, ["Wq%d" % ws], "qT")
                for j in range(4):
                    tsl = slice(j * 128, (j + 1) * 128)
                    MM(PS[1][:, tsl], kT[s][:, tsl], qT[:, tsl], True, True, ["kT%d" % s, "qT"], ["ps1"])
                TT_(ATm, PS[1].rearrange("p (j t) -> p j t", t=128), trif.unsqueeze(1).to_broadcast([128, 4, 128]), ALU.mult,
                    ["ps1", "trif"], ["ATm"])

            def C_part(h, i, s, ws):
                p = s
                st = stt[p]
                for j in range(4):
                    c = PRE_C + i * 4 + j
                    sj = j % 2
                    tsl = slice(j * 128, (j + 1) * 128)
                    bo = (2, 0)[j % 2]
                    for kc in range(KC):
                        MM(PS[bo], xTt[s][:, kc, 3 + j * 128:3 + (j + 1) * 128], Woz[ws][:, kc, :],
                           kc == 0, kc == KC - 1, ["xTt%d" % s, "Woz%d" % ws], ["ps%d" % bo])
                    ACT(to[sj], PS[bo][:, 0:256], AF.Tanh, ["ps%d" % bo], ["to%d" % sj], scale=0.5)
                    ACT(szz[sj], PS[bo][:, 256:512], AF.Silu, ["ps%d" % bo], ["szz%d" % sj])
                    STT(y2[p][j], to[sj], 1.0, szz[sj], ALU.add, ALU.mult, ["to%d" % sj, "szz%d" % sj], ["y2_%d_%d" % (p, j)])
                    first = (i == 0 and j == 0)
                    aprev = 1.0 if first else v3(aL)[:, c - 1, h:h + 1]
                    if first:
                        CP(Sbf[:, 0, 0:257], S[:, h, 0:257], ["S"], ["Sbf0"])
                    bn = 4 + j
                    MM(PS[bn][:, 0:257], ATm[:, j, :], vp[s][:, j, 0:257], True, False, ["ATm", "vp%d" % s], ["ps%d" % bn])
                    MM(PS[bn][:, 0:257], qT[:, tsl], Sbf[:, sj, 0:257], False, True, ["qT", "Sbf%d" % sj], ["ps%d" % bn])
                    MM(PS[3][:, 0:257], ktok[s][:, j, :], vp[s][:, j, 0:257], True, True, ["ktok%d" % s, "vp%d" % s], ["ps3"])
                    STT(S[:, h, 0:257], S[:, h, 0:257], aprev, PS[3][:, 0:257], ALU.mult, ALU.add,
                        ["ps3", "aL", "S"], ["S"])
                    if j < 3 or i < TT - 1:
                        TS(Sbf[:, (j + 1) % 2, 0:257], S[:, h, 0:257], v3(aL)[:, c, h:h + 1], None, ALU.mult, None,
                           ["S", "aL"], ["Sbf%d" % ((j + 1) % 2)])
                    ACT(st[:, 32 + j:33 + j], PS[bn][:, 256:257], AF.Abs, ["ps%d" % bn, "ee"], ["st_ab%d_%d" % (p, j)],
                        scale=v3(ee)[:, c, h:h + 1])
                    TS(st[:, j:j + 1], st[:, 32 + j:33 + j], 1.0, None, ALU.max, None, ["st_ab%d_%d" % (p, j)], ["st_dd%d_%d" % (p, j)])
                    RECIP(st[:, 4 + j:5 + j], st[:, j:j + 1], ["st_dd%d_%d" % (p, j)], ["st_r%d_%d" % (p, j)])
                    TT_(st[:, 8 + j:9 + j], st[:, 4 + j:5 + j], v3(ee)[:, c, h:h + 1], ALU.mult, ["st_r%d_%d" % (p, j), "ee"],
                        ["st_r2_%d_%d" % (p, j)])
                    ACT(jk, PS[bn][:, 0:256], AF.Square, ["ps%d" % bn, "st_r2_%d_%d" % (p, j)], ["st_ssq%d_%d" % (p, j)],
                        scale=st[:, 8 + j:9 + j], accum_out=st[:, 12 + j:13 + j])

                ssqk = ["st_ssq%d_%d" % (p, j) for j in range(4)]
                TS(st[:, 16:20], st[:, 12:16], 1.0 / 256, EPS, ALU.mult, ALU.add, ssqk, ["st_v%d" % p])
                RSQRT(st[:, 24:28], st[:, 16:20], ["st_v%d" % p], ["st_rs%d" % p])
                TT_(st[:, 28:32], st[:, 24:28], st[:, 8:12], ALU.mult, ["st_rs%d" % p] + ["st_r2_%d_%d" % (p, j) for j in range(4)],
                    ["st_sc%d" % p], eng="pool")

            def E_part(h, i, s, ws):
                p = s
                st = stt[p]
                for j in range(4):
                    bn = 4 + j
                    yb = ybf[j % 2]
                    STT(y1, PS[bn][:, 0:256], st[:, 28 + j:29 + j], mnw[ws], ALU.mult, ALU.mult,
                        ["ps%d" % bn, "st_sc%d" % p, "mnw%d" % ws], ["y1"])
                    STT(yb, y1, 0.5, y2[p][j], ALU.mult, ALU.mult, ["y1", "y2_%d_%d" % (p, j)], ["ybf%d" % (j % 2)])
                    for vb in range(2):
                        r = vb * 4 + j
                        P.op("pe", lambda e, o=PSB[0][:, r * 128:(r + 1) * 128], a=yb[:, vb * 128:(vb + 1) * 128]:
                             e.transpose(o, a, ident), ["ybf%d" % (j % 2), "ident"], ["ps0"])
                CP(yT, PSB[0], ["ps0"], ["yT"], eng="act")
                for vb in range(2):
                    fc = 2 * h + vb
                    DMA("sp", yT_d[i * 4:(i + 1) * 4, :, fc, :].rearrange("c p t -> p c t"),
                        yT[:, vb * 512:(vb + 1) * 512].rearrange("p (c t) -> p c t", t=128), ["yT"], [], "yTst3_%d" % vb)

            loadW(0)
            load_tile(0)
            it = 0
            prev = None
            for h in range(NH):
                ws = h % 2
                for i in range(TT):
                    s = it % 2
                    if it + 1 < NH * TT:
                        load_tile(it + 1)
                    QA_part(h, i, s, ws)
                    if prev is not None:
                        E_part(*prev)
                    if i == 0 and h + 1 < NH:
                        loadW(h + 1)
                    C_part(h, i, s, ws)
                    prev = (h, i, s, ws)
                    it += 1
            E_part(*prev)

        phase3()
        P.barrier()
        A.reset(pm_p4)

        def phase4():
            wo = A.bf16(32 * D).rearrange("p (f c) -> p f c", c=D)
            fnw = A.f32(D)
            yTt = [A.bf16(32 * 128).rearrange("p (f t) -> p f t", t=128) for _ in range(2)]
            xr = [A.f32(D) for _ in range(2)]
            junk4 = A.bf16(D)
            s4 = A.f32(8)
            for f0 in range(0, 32, 4):
                DMA("pool", wo[:, f0:f0 + 4, :], wout_v[:, f0:f0 + 4, :], [], ["wo"], "wo")
            DMA("sp", fnw, fnw_d.partition_broadcast(128), [], ["fnw"], "fnw")

            def load(c):
                s_ = c % 2
                DMA("sp", yTt[s_], yT_d[c], [], ["yTt%d" % s_], "yTt%d" % s_)
                DMA("sp", xr[s_], x_d[c * 128:(c + 1) * 128, :], [], ["xr%d" % s_], "xr%d" % s_)

            load(0)
            for c in range(NCH):
                s = c % 2
                if c + 1 < NCH:
                    load(c + 1)
                bb = 4 * (c % 2)
                for fc in range(32):
                    for n in range(4):
                        MM(PS[bb + n], yTt[s][:, fc, :], wo[:, fc, n * 512:(n + 1) * 512], fc == 0, fc == 31,
                           ["yTt%d" % s, "wo"], ["ps%d" % (bb + n)])
                for n in range(4):
                    TT_(xr[s][:, n * 512:(n + 1) * 512], PS[bb + n], xr[s][:, n * 512:(n + 1) * 512], ALU.add,
                        ["ps%d" % (bb + n), "xr%d" % s], ["xr%d" % s])
                ACT(junk4, xr[s], AF.Square, ["xr%d" % s], ["junk4", "s4a"], accum_out=s4[:, 0:1])
                TS(s4[:, 1:2], s4[:, 0:1], 1.0 / D, EPS, ALU.mult, ALU.add, ["s4a"], ["s4b"])
                RSQRT(s4[:, 3:4], s4[:, 1:2], ["s4b"], ["s4d"])
                STT(xr[s], xr[s], s4[:, 3:4], fnw, ALU.mult, ALU.mult, ["xr%d" % s, "s4d", "fnw"], ["xr%d" % s])
                DMA("sp", out_d[c * 128:(c + 1) * 128, :], xr[s], ["xr%d" % s], ["out%d" % s], "ost%d" % s)

        phase4()
        P.final_wait("sp", ["out0", "out1"])
        P.emit()
    return nc


_NC_CACHE = {}


def kernel(x, norm_w, w_in, conv_w, conv_b, b_igate, b_fgate, mlstm_norm_w,
           sgu_norm_w, sgu_norm_b, w_spatial, b_spatial, w_out, final_norm_w):
    f = lambda a: np.ascontiguousarray(np.asarray(a, dtype=np.float32))
    x = f(x)
    B, S_, _ = x.shape
    assert B * NSEG == NCORES and S_ == NSEG * NTOK
    if "nc" not in _NC_CACHE:
        _NC_CACHE["nc"] = build_program()
    nc = _NC_CACHE["nc"]
    conv_wT = f(np.asarray(conv_w).reshape(4, 16, 128).transpose(2, 1, 0).reshape(128, 64))
    conv_bT = f(np.asarray(conv_b).reshape(16, 128).T)
    b_if = f(np.concatenate([np.asarray(b_igate), np.asarray(b_fgate)]))
    w_spT = f(np.asarray(w_spatial).transpose(0, 2, 1))
    ident = np.eye(128, dtype=np.float32)
    tri = np.triu(np.ones((128, 128), dtype=np.float32))
    ones = np.ones((128, 128), dtype=np.float32)
    shared = {
        "norm_w": f(norm_w), "w_in": f(w_in), "conv_wT": conv_wT, "conv_bT": conv_bT, "b_if": b_if,
        "mlstm_norm_w": f(mlstm_norm_w), "sgu_norm_w": f(sgu_norm_w), "sgu_norm_b": f(sgu_norm_b),
        "w_spT": w_spT, "b_spatial": f(b_spatial), "w_out": f(w_out), "final_norm_w": f(final_norm_w),
        "ident": ident, "tri": tri, "ones": ones,
    }
    in_maps = []
    for c in range(NCORES):
        b, sg = divmod(c, NSEG)
        t0 = sg * NTOK
        xs = x[b, t0:t0 + NTOK]
        xh = x[b, t0 - 128:t0] if sg > 0 else np.zeros((128, D), np.float32)
        m = dict(shared)
        m["x"] = np.ascontiguousarray(xs)
        m["xh"] = np.ascontiguousarray(xh)
        xe = np.zeros((PRE_C * 128, D), np.float32)
        if t0 > 0:
            xe[PRE_C * 128 - t0:] = x[b, 0:t0]
        mch = np.zeros(NCHX, np.float32)
        mch[PRE_C - t0 // 128:] = 1.0
        m["xe"] = xe
        m["mch"] = mch
        m["xh"] = np.zeros((128, D), np.float32)
        in_maps.append(m)
    res = run_bass_kernel_spmd(nc, in_maps, core_ids=list(range(NCORES)))
    out = np.empty((B, S_, D), dtype=np.float32)
    for c in range(NCORES):
        b, sg = divmod(c, NSEG)
        out[b, sg * NTOK:(sg + 1) * NTOK] = res.results[c]["out"]
    return out
```

```python
import contextlib
import numpy as np
import concourse.bass as bass
import concourse.mybir as mybir
from concourse.bass_utils import run_bass_kernel_spmd

F32 = mybir.dt.float32
BF16 = mybir.dt.bfloat16
AF = mybir.ActivationFunctionType
ALU = mybir.AluOpType
AX = mybir.AxisListType

D = 2048
NH = 8
DQK = 1024
DIN = 14352
DMIX = 4096
KC = 16
OFF_Q, OFF_K, OFF_V, OFF_O, OFF_Z, OFF_I, OFF_F, OFF_U, OFF_VG, OFF_ZG = (
    0, 1024, 2048, 4096, 6144, 8192, 8200, 8208, 10256, 12304)
EPS = 1e-6
NCORES = 8
NSEG = 4
NTOK = 16384 // NSEG
NCH = NTOK // 128
TT = NTOK // 512
VW = 258
NCHX = 16384 // 128
TTX = 16384 // 512
PRE_T = TTX - TT
PRE_C = NCHX - NCH
USE_CC = False
SAME_ENGINE_SYNC = True


class Prog:
    def __init__(self, nc, same_engine_sync=True):
        self.nc = nc
        self.ops = []
        self.cnt = {e: 0 for e in ("pe", "act", "dve", "pool", "sp")}
        self.res = {}
        self.waited = {}
        self.dma_cnt = {}
        self.same_engine_sync = same_engine_sync
        self.sem_keys = []

    def _semkey(self, k):
        if k not in self.sem_keys:
            self.sem_keys.append(k)
        return k

    def _deps(self, reads, writes):
        deps = []
        for k in reads:
            r = self.res.get(k)
            if r and r["w"] is not None:
                deps.append(r["w"])
        for k in writes:
            r = self.res.get(k)
            if r:
                if r["w"] is not None:
                    deps.append(r["w"])
                deps.extend(r["r"])
        return deps

    def _commit(self, ev, reads, writes):
        for k in reads:
            r = self.res.setdefault(k, {"w": None, "r": []})
            r["r"].append(ev)
        for k in writes:
            self.res[k] = {"w": ev, "r": []}

    def op(self, engine, fn, reads=(), writes=(), dma=None):
        deps = self._deps(reads, writes)
        waits = {}
        for (sk, val, eng_of_ev) in deps:
            if eng_of_ev == engine:
                if engine == "pe" or not self.same_engine_sync:
                    continue
            if self.waited.get((engine, sk), 0) >= val:
                continue
            waits[sk] = max(waits.get(sk, 0), val)
        for sk, val in waits.items():
            self.waited[(engine, sk)] = val
        if dma is not None:
            sk = self._semkey(("dma", dma))
            self.dma_cnt[sk] = self.dma_cnt.get(sk, 0) + 16
            ev = (sk, self.dma_cnt[sk], None)
            inc = (sk, 16)
        else:
            sk = self._semkey(("eng", engine))
            self.cnt[engine] += 1
            ev = (sk, self.cnt[engine], engine)
            inc = (sk, 1)
        self._commit(ev, reads, writes)
        self.ops.append((engine, fn, sorted(waits.items(), key=lambda x: str(x[0])), inc))
        return ev

    def barrier(self):
        allw = {}
        for k in self.sem_keys:
            v = self.cnt[k[1]] if k[0] == "eng" else self.dma_cnt.get(k, 0)
            if v > 0:
                allw[k] = v
        for e in ("sp", "act", "dve", "pool", "pe"):
            w = {}
            for k, v in allw.items():
                if k == ("eng", e):
                    continue
                if self.waited.get((e, k), 0) >= v:
                    continue
                w[k] = v
                self.waited[(e, k)] = v
            if w:
                self.ops.append((e, None, sorted(w.items(), key=lambda x: str(x[0])), None))
        self.res = {}

    def final_wait(self, engine, keys):
        deps = self._deps(keys, ())
        waits = {}
        for (sk, val, _e) in deps:
            waits[sk] = max(waits.get(sk, 0), val)
        self.ops.append((engine, None, sorted(waits.items(), key=lambda x: str(x[0])), None))

    def emit(self):
        nc = self.nc
        with contextlib.ExitStack() as st:
            sems = {}
            for i, k in enumerate(self.sem_keys):
                sems[k] = st.enter_context(nc.semaphore("s%d" % i))
            block = st.enter_context(nc.Block())
            ops = self.ops

            def run(engname, eng):
                for (e, fn, waits, inc) in ops:
                    if e != engname:
                        continue
                    for sk, val in waits:
                        eng.wait_ge(sems[sk], val)
                    if fn is None:
                        continue
                    ins = fn(eng)
                    if inc is not None:
                        ins.then_inc(sems[inc[0]], inc[1])

            @block.sync
            def _(eng):
                run("sp", eng)

            @block.scalar
            def _(eng):
                run("act", eng)

            @block.vector
            def _(eng):
                run("dve", eng)

            @block.gpsimd
            def _(eng):
                run("pool", eng)

            @block.tensor
            def _(eng):
                run("pe", eng)


class Arena:
    def __init__(self, ap, words):
        self.ap, self.words, self.off = ap, words, 0

    def mark(self):
        return self.off

    def reset(self, m):
        self.off = m

    def f32(self, n):
        a = self.ap[:, self.off:self.off + n]
        self.off += n
        assert self.off <= self.words, ("arena overflow", self.off, self.words)
        return a

    def bf16(self, n):
        w = (n + 1) // 2
        a = self.ap[:, self.off:self.off + w].bitcast(BF16)
        self.off += w
        assert self.off <= self.words, ("arena overflow", self.off, self.words)
        return a[:, 0:n]


def build_program():
    nc = bass.Bass("TRN2", target_bir_lowering=False)
    dt_in = lambda name, shape: nc.dram_tensor(name, shape, F32, kind="ExternalInput").ap()
    x_d = dt_in("x", [NTOK, D])
    xh_d = dt_in("xh", [128, D])
    nw_d = dt_in("norm_w", [D])
    win_d = dt_in("w_in", [D, DIN])
    cw_d = dt_in("conv_wT", [128, 64])
    cb_d = dt_in("conv_bT", [128, 16])
    bif_d = dt_in("b_if", [16])
    mnw_d = dt_in("mlstm_norm_w", [D])
    sgw_d = dt_in("sgu_norm_w", [D])
    sgb_d = dt_in("sgu_norm_b", [D])
    wsp_d = dt_in("w_spT", [NH, 128, 128])
    bsp_d = dt_in("b_spatial", [NH, 128])
    wout_d = dt_in("w_out", [DMIX, D])
    fnw_d = dt_in("final_norm_w", [D])
    idn_d = dt_in("ident", [128, 128])
    tri_d = dt_in("tri", [128, 128])
    ones_d = dt_in("ones", [128, 128])
    xe_d = dt_in("xe", [PRE_C * 128, D])
    mch_d = dt_in("mch", [NCHX])
    out_d = nc.dram_tensor("out", [NTOK, D], F32, kind="ExternalOutput").ap()
    xT_d = nc.dram_tensor("xT_s", [128, KC, (NCHX + 1) * 128], BF16).ap()
    kT_d = nc.dram_tensor("kT_s", [NH, 128, NTOK], BF16).ap()
    kt_d = nc.dram_tensor("kt_s", [NH, NCH, 128, 128], BF16).ap()
    v_d = nc.dram_tensor("v_s", [NH, NCH, 128, VW], BF16).ap()
    yT_d = nc.dram_tensor("yT_s", [NCH, 128, 32, 128], BF16).ap()

    win_v = win_d.rearrange("(kc p) c -> p kc c", p=128)
    wout_v = wout_d.rearrange("(fc p) c -> p fc c", p=128)

    with contextlib.ExitStack() as st:
        ARENA_WORDS = 51200
        arena_t = st.enter_context(nc.sbuf_tensor("arena", [128, ARENA_WORDS], F32))
        A = Arena(arena_t[:, :], ARENA_WORDS)
        banks = [st.enter_context(nc.psum_tensor("bank%d" % i, [128, 512], F32)) for i in range(8)]
        PS = [b[:, :] for b in banks]
        PSB = [b[:, :].bitcast(BF16) for b in banks]
        P = Prog(nc, same_engine_sync=SAME_ENGINE_SYNC)

        def MM(out, lhsT, rhs, start, stop, R, W):
            P.op("pe", lambda e: e.matmul(out, lhsT=lhsT, rhs=rhs, start=start, stop=stop), R, W)

        def ACT(out, in_, func, R, W, **kw):
            P.op("act", lambda e: e.activation(out=out, in_=in_, func=func, **kw), R, W)

        def DMA(q, out, in_, R, W, key):
            P.op(q, lambda e: e.dma_start(out=out, in_=in_), R, W, dma=key)

        def TS(out, in0, s1, s2, op0, op1, R, W, eng="dve"):
            if op1 is None:
                P.op(eng, lambda e: e.tensor_scalar(out, in0, s1, None, op0), R, W)
            else:
                P.op(eng, lambda e: e.tensor_scalar(out, in0, s1, s2, op0, op1), R, W)

        def STT(out, in0, scalar, in1, op0, op1, R, W, eng="dve"):
            P.op(eng, lambda e: e.scalar_tensor_tensor(out=out, in0=in0, scalar=scalar, in1=in1, op0=op0, op1=op1), R, W)

        def TT_(out, in0, in1, op, R, W, eng="dve"):
            P.op(eng, lambda e: e.tensor_tensor(out, in0, in1, op), R, W)

        def CP(out, in_, R, W, eng="dve"):
            if eng == "act":
                P.op("act", lambda e: e.copy(out, in_), R, W)
            else:
                P.op(eng, lambda e: e.tensor_copy(out, in_), R, W)

        def RSQRT(out, in_, R, W):
            n_ = out.shape[-1]
            P.op("pool", lambda e: e.tensor_tensor(out, in_, mhalf[:, 0:n_], ALU.pow), R + ["mhalf"], W)

        def RECIP(out, in_, R, W):
            P.op("dve", lambda e: e.reciprocal(out, in_), R, W)

        identf = A.f32(128)
        trif = A.f32(128)
        onesf = A.f32(128)
        ident = A.bf16(128)
        cw = A.f32(64).rearrange("p (c j) -> p c j", j=4)
        cb = A.f32(16)
        bif = A.f32(16)
        S = A.f32(NH * VW).rearrange("p (h v) -> p h v", v=VW)
        Sinit = A.f32(NH * VW).rearrange("p (h v) -> p h v", v=VW)
        Sbf = A.bf16(2 * VW).rearrange("p (s v) -> p s v", v=VW)
        small = A.f32(64)
        mhalf = A.f32(16)
        P.op("pool", lambda e: e.memset(mhalf, -0.5), [], ["mhalf"])
        pm_p4 = A.mark()
        gpre = A.f32(NCHX * 16).rearrange("p (c g) -> p c g", g=16)
        NG = NCHX * 8
        li = A.f32(NG)
        spl = A.f32(NG)
        cs = A.f32(NG)
        gg = A.f32(NG)
        ee = A.f32(NG)
        aL = A.f32(NG)
        mch = A.f32(NCHX)
        v3 = lambda a: a.rearrange("p (c h) -> p c h", h=8)
        DMA("sp", identf, idn_d, [], ["identf"], "c0")
        DMA("sp", trif, tri_d, [], ["trif"], "c1")
        DMA("sp", onesf, ones_d, [], ["onesf"], "c2")
        DMA("sp", cw.rearrange("p c j -> p (c j)"), cw_d, [], ["cw"], "c3")
        DMA("sp", cb, cb_d, [], ["cb"], "c4")
        DMA("sp", bif, bif_d.partition_broadcast(128), [], ["bif"], "c5")
        DMA("sp", mch, mch_d.partition_broadcast(128), [], ["mch"], "c7")
        CP(ident, identf, ["identf"], ["ident"])
        pm_persist = A.mark()

        nw = A.f32(D)
        wif = A.bf16(KC * 16).rearrange("p (k c) -> p k c", c=16)
        NB = 8
        GP = 4
        xt = [A.f32(D) for _ in range(NB)]
        xn = [A.bf16(D) for _ in range(GP)]
        junk = A.bf16(D)
        xTs = [A.bf16(KC * 512).rearrange("p (k t) -> p k t", t=512) for _ in range(2)]
        st0 = A.f32(32)
        DMA("sp", nw, nw_d.partition_broadcast(128), [], ["nw"], "c6")
        DMA("pool", wif, win_v[:, :, OFF_I:OFF_I + 16], [], ["wif"], "wif")
        groups0 = [list(range(g, min(g + GP, NCHX + 1))) for g in range(0, NCHX + 1, GP)]

        def p0_stageA(gi):
            cs_ = groups0[gi]
            p = gi % 2
            base = p * 16
            n = len(cs_)
            for k, c in enumerate(cs_):
                s = c % NB
                src = xh_d if c == 0 else (xe_d[(c - 1) * 128:c * 128, :] if c <= PRE_C
                                           else x_d[(c - 1 - PRE_C) * 128:(c - PRE_C) * 128, :])
                DMA("sp", xt[s], src, [], ["xt%d" % s], "xt%d" % s)
                ACT(junk, xt[s], AF.Square, ["xt%d" % s], ["ssq%d_%d" % (p, k)], accum_out=st0[:, base + k:base + k + 1])
            TS(st0[:, base + 4:base + 4 + n], st0[:, base:base + n], 1.0 / D, EPS, ALU.mult, ALU.add,
               ["ssq%d_%d" % (p, k) for k in range(n)], ["sv%d" % p])
            RSQRT(st0[:, base + 12:base + 12 + n], st0[:, base + 4:base + 4 + n], ["sv%d" % p], ["rstd%d" % p])

        def p0_stageB(gi):
            cs_ = groups0[gi]
            p = gi % 2
            base = p * 16
            n = len(cs_)
            xs = xTs[p]
            for k, c in enumerate(cs_):
                s = c % NB
                STT(xn[k], xt[s], st0[:, base + 12 + k:base + 13 + k], nw, ALU.mult, ALU.mult,
                    ["xt%d" % s, "rstd%d" % p, "nw"], ["xn%d" % k])
            for k, c in enumerate(cs_):
                for half in range(2):
                    b = (2 * c + half) % 6
                    for j in range(8):
                        kc = half * 8 + j
                        P.op("pe", lambda e, o=PSB[b][:, j * 128:(j + 1) * 128], i=xn[k][:, kc * 128:(kc + 1) * 128]:
                             e.transpose(o, i, ident), ["xn%d" % k, "ident"], ["ps%d" % b])
                    CP(xs[:, half * 8:(half + 1) * 8, k * 128:(k + 1) * 128], PSB[b].rearrange("p (k t) -> p k t", t=128),
                       ["ps%d" % b], ["xTs%d_%d_%d" % (p, k, half)], eng=("act" if half == 0 else "dve"))
            allk = ["xTs%d_%d_%d" % (p, k, half) for k in range(n) for half in range(2)]
            c0 = cs_[0]
            DMA("sp", xT_d[:, :, c0 * 128:(c0 + n) * 128], xs[:, :, 0:n * 128], allk, [], "xTst%d" % p)
            for k, c in enumerate(cs_):
                if c >= 1:
                    bg = 6 + (c % 2)
                    for kc in range(KC):
                        MM(PS[bg][:, 0:16], xs[:, kc, k * 128:(k + 1) * 128], wif[:, kc, :], kc == 0, kc == KC - 1,
                           ["xTs%d_%d_0" % (p, k), "xTs%d_%d_1" % (p, k), "wif"], ["ps%d" % bg])
                    CP(gpre[:, c - 1, :], PS[bg][:, 0:16], ["ps%d" % bg], ["gpre%d" % (c % 2)], eng="act")

        p0_stageA(0)
        for gi in range(len(groups0)):
            if gi + 1 < len(groups0):
                p0_stageA(gi + 1)
            p0_stageB(gi)
        for c in range(NCHX):
            TT_(v3(li)[:, c, :], gpre[:, c, 0:8], bif[:, 0:8], ALU.add, ["gpre0", "gpre1", "bif"], ["li"])
            TT_(v3(spl)[:, c, :], gpre[:, c, 8:16], bif[:, 8:16], ALU.add, ["gpre0", "gpre1", "bif"], ["spl"])
        ACT(spl, spl, AF.Exp, ["spl"], ["spl"], scale=-1.0)
        ACT(spl, spl, AF.Ln, ["spl"], ["spl"], bias=1.0)
        for o in range(0, NG, 512):
            n = min(512, NG - o)
            MM(PS[6][:, 0:n], trif, spl[:, o:o + n], True, True, ["trif", "spl"], ["ps6"])
            CP(cs[:, o:o + n], PS[6][:, 0:n], ["ps6"], ["cs"])
            MM(PS[7][:, 0:n], onesf, spl[:, o:o + n], True, True, ["onesf", "spl"], ["ps7"])
            ACT(aL[:, o:o + n], PS[7][:, 0:n], AF.Exp, ["ps7"], ["aL"], scale=-1.0)
        ACT(ee, cs, AF.Exp, ["cs"], ["ee"], scale=-1.0)
        TT_(gg, li, cs, ALU.add, ["li", "cs"], ["gg"])
        P.op("dve", lambda e: e.memset(small[:, 8:9], -0.5 * float(np.log(128.0))), [], ["lnsc"])
        ACT(gg, gg, AF.Exp, ["gg", "lnsc"], ["gg"], bias=small[:, 8:9])
        TT_(v3(gg), v3(gg), mch.unsqueeze(2).to_broadcast([128, NCHX, 8]), ALU.mult, ["gg", "mch"], ["gg"])
        gA = li
        TT_(gA, gg, aL, ALU.mult, ["gg", "aL", "li"], ["gA"])
        P.barrier()
        A.reset(pm_persist)

        def load_xT(buf, slot, i):
            t0 = 128 + 512 * i
            DMA("sp", buf, xT_d[:, :, t0 - 3:t0 + 512], [], ["xTt%d" % slot], "xTt%d" % slot)

        def feat_proj_conv(W, xTt, slot, fcidx, pre, acc, outT, bmain, bhalo, wkeys, outkey):
            fpc_mm(W, xTt, slot, pre, bmain, PS[bhalo][:, 0:3], "ps%d" % bhalo, wkeys)
            fpc_conv(fcidx, pre, acc, outT, outkey)

        def fpc_mm(W, xTt, slot, pre, bmain, halo_ap, halo_key, wkeys):
            for kc in range(KC):
                MM(PS[bmain], W[:, kc, :], xTt[:, kc, 3:515], kc == 0, kc == KC - 1,
                   wkeys + ["xTt%d" % slot], ["ps%d" % bmain])
            for kc in range(KC):
                MM(halo_ap, W[:, kc, :], xTt[:, kc, 0:3], kc == 0, kc == KC - 1,
                   wkeys + ["xTt%d" % slot], [halo_key])
            CP(pre[:, 3:515], PS[bmain], ["ps%d" % bmain], ["pre_m"], eng="act")
            CP(pre[:, 0:3], halo_ap, [halo_key], ["pre_h"], eng="act")

        def fpc_conv(fcidx, pre, acc, outT, outkey):
            TS(acc, pre[:, 3:515], cw[:, fcidx, 3:4], cb[:, fcidx:fcidx + 1], ALU.mult, ALU.add,
               ["pre_m", "cw", "cb"], ["acc"])
            for j in range(3):
                STT(acc, pre[:, j:j + 512], cw[:, fcidx, j:j + 1], acc, ALU.mult, ALU.add,
                    ["pre_m", "pre_h", "cw", "acc"], ["acc"])
            ACT(outT, acc, AF.Silu, ["acc"], [outkey])

        def phase1():
            WkA = A.bf16(KC * 1024).rearrange("p (k c) -> p k c", c=1024)
            WvA = A.bf16(KC * 2048).rearrange("p (k c) -> p k c", c=2048)
            Wk = [WkA[:, :, h_ * 128:(h_ + 1) * 128] for h_ in range(NH)]
            Wv = [WvA[:, :, h_ * 256:(h_ + 1) * 256] for h_ in range(NH)]
            xTt = [A.bf16(KC * 515).rearrange("p (k t) -> p k t", t=515) for _ in range(2)]
            pre = A.f32(515)
            acc = A.f32(512)
            kT = [A.bf16(512) for _ in range(2)]
            ktok = [A.bf16(512).rearrange("p (j d) -> p j d", d=128) for _ in range(2)]
            vp = [A.bf16(4 * VW).rearrange("p (j v) -> p j v", v=VW) for _ in range(2)]
            P.op("dve", lambda e: e.memset(S.rearrange("p h v -> p (h v)"), 0.0), [], ["S"])

            def loadW(h):
                DMA("pool", Wk[h], win_v[:, :, OFF_K + h * 128:OFF_K + (h + 1) * 128], [], ["Wk%d" % h], "Wk%d" % h)
                DMA("pool", Wv[h], win_v[:, :, OFF_V + h * 256:OFF_V + (h + 1) * 256], [], ["Wv%d" % h], "Wv%d" % h)

            def K_mm(h, i, s, ws, xs_):
                fpc_mm(Wk[ws], xTt[xs_], xs_, pre, 0, PS[4][:, 300:303], "ps4", ["Wk%d" % ws])

            def K_conv(h, i, s, ws, xs_):
                fpc_conv(8 + h, pre, acc, kT[s], "kT%d" % s)
                if i >= PRE_T:
                    io = i - PRE_T
                    DMA("sp", kT_d[h, :, io * 512:(io + 1) * 512], kT[s], ["kT%d" % s], [], "kTst%d" % s)

            def V_part(h, i, s, ws, xs_, js):
                for j in js:
                    c = i * 4 + j
                    b = 2 + (j % 2)
                    for kc in range(KC):
                        MM(PS[b][:, 0:256], xTt[xs_][:, kc, 3 + j * 128:3 + (j + 1) * 128], Wv[ws][:, kc, :],
                           kc == 0, kc == KC - 1, ["xTt%d" % xs_, "Wv%d" % ws], ["ps%d" % b])
                    sc_ = (v3(gg) if i >= PRE_T else v3(gA))[:, c, h:h + 1]
                    ACT(vp[s][:, j, 0:256], PS[b][:, 0:256], AF.Identity, ["ps%d" % b, "gg", "gA"], ["vp%d_%d" % (s, j)],
                        scale=sc_)
                    ACT(vp[s][:, j, 256:257], onesf[:, 0:1], AF.Identity, ["onesf", "gg", "gA", "vp%d_%d" % (s, j)],
                        ["vp%d_%d" % (s, j)], scale=sc_)

            def T_part(h, i, s):
                for j in range(4):
                    P.op("pe", lambda e, o=PSB[4][:, j * 128:(j + 1) * 128], a=kT[s][:, j * 128:(j + 1) * 128]:
                         e.transpose(o, a, ident), ["kT%d" % s, "ident"], ["ps4"])
                CP(ktok[s].rearrange("p j d -> p (j d)"), PSB[4][:, 0:512], ["ps4"], ["ktok%d" % s], eng="act")

            def S_part(h, i, s):
                if i < PRE_T:
                    for j in range(4):
                        c = i * 4 + j
                        bs = (1, 5, 6, 7)[j]
                        MM(PS[bs][:, 0:257], ktok[s][:, j, :], vp[s][:, j, 0:257], True, True,
                           ["ktok%d" % s, "vp%d_%d" % (s, j)], ["ps%d" % bs])
                        STT(S[:, h, 0:257], S[:, h, 0:257], v3(aL)[:, c, h:h + 1], PS[bs][:, 0:257], ALU.mult, ALU.add,
                            ["ps%d" % bs, "aL", "S"], ["S"])
                    if i == PRE_T - 1:
                        CP(Sinit[:, h, :], S[:, h, :], ["S"], ["Sinit"])
                else:
                    io = i - PRE_T
                    DMA("sp", kt_d[h, io * 4:(io + 1) * 4, :, :].rearrange("c p d -> p c d"), ktok[s],
                        ["ktok%d" % s], [], "ktst%d" % s)
                    DMA("sp", v_d[h, io * 4:(io + 1) * 4, :, :].rearrange("c p v -> p c v"), vp[s],
                        ["vp%d_%d" % (s, j) for j in range(4)], [], "vst%d" % s)

            load_xT(xTt[0], 0, 0)
            for h in range(NH):
                loadW(h)
            it = 0
            prev = None
            for i in range(TTX):
                xs_ = i % 2
                if i + 1 < TTX:
                    load_xT(xTt[(i + 1) % 2], (i + 1) % 2, i + 1)
                for h in range(NH):
                    s = it % 2
                    if prev is not None:
                        T_part(*prev)
                    K_mm(h, i, s, h, xs_)
                    V_part(h, i, s, h, xs_, (0, 1))
                    K_conv(h, i, s, h, xs_)
                    V_part(h, i, s, h, xs_, (2, 3))
                    if prev is not None:
                        S_part(*prev)
                    prev = (h, i, s)
                    it += 1
            T_part(*prev)
            S_part(*prev)

        phase1()
        P.barrier()
        A.reset(pm_persist)
        pm2 = A.mark()

        def phase2():
            W = [A.bf16(KC * 768).rearrange("p (k c) -> p k c", c=768) for _ in range(2)]
            xTt = [A.bf16(KC * 515).rearrange("p (k t) -> p k t", t=515) for _ in range(2)]
            wsf = A.f32(128)
            wsb = [A.bf16(128) for _ in range(2)]
            bsph = [A.f32(512) for _ in range(2)]
            sgw = [A.f32(256) for _ in range(2)]
            sgb = [A.f32(256) for _ in range(2)]
            usb2 = [A.f32(512) for _ in range(2)]
            sq2 = [A.f32(512) for _ in range(2)]
            th2 = [A.f32(512) for _ in range(2)]
            gctr = [0]
            gu = [A.f32(512) for _ in range(2)]
            sz = [A.f32(512) for _ in range(2)]
            gv = A.f32(1024).rearrange("p (j c) -> p j c", c=256)
            sqv = A.f32(1024).rearrange("p (j c) -> p j c", c=256)
            gvn = A.f32(1024).rearrange("p (j c) -> p j c", c=256)
            vgn = A.bf16(1024).rearrange("p (j c) -> p j c", c=256)
            st4 = A.f32(32)
            tmp = A.f32(512)
            yT = [A.bf16(512) for _ in range(2)]

            def loadW(h):
                s = h % 2
                for g, off in enumerate((OFF_U, OFF_VG, OFF_ZG)):
                    DMA("pool", W[s][:, :, g * 256:(g + 1) * 256], win_v[:, :, off + h * 256:off + (h + 1) * 256],
                        [], ["W%d" % s], "W2_%d" % s)
                DMA("sp", wsf, wsp_d[h], [], ["wsf"], "wsf")
                TT_(wsb[s], wsf, trif, ALU.mult, ["wsf", "trif"], ["wsb%d" % s])
                for r in range(4):
                    DMA("sp", bsph[s][:, r * 128:(r + 1) * 128], bsp_d[h].partition_broadcast(128), [], ["bsph%d" % s], "bsp%d" % s)
                TS(bsph[s], bsph[s], 0.5, None, ALU.mult, None, ["bsph%d" % s], ["bsph%d" % s])
                DMA("sp", sgw[s], sgw_d[h * 256:(h + 1) * 256].partition_broadcast(128), [], ["sgw%d" % s], "sgw%d" % s)
                DMA("sp", sgb[s], sgb_d[h * 256:(h + 1) * 256].partition_broadcast(128), [], ["sgb%d" % s], "sgb%d" % s)

            def gelu_s1(ps_ap, pskey, n):
                q = gctr[0] % 2
                gctr[0] += 1
                usb, sq = usb2[q], sq2[q]
                ku, ks = "usb%d" % q, "sq%d" % q
                CP(usb[:, 0:n], ps_ap, [pskey], [ku], eng="act")
                ACT(sq[:, 0:n], ps_ap, AF.Square, [pskey], [ks])
                TS(sq[:, 0:n], sq[:, 0:n], 0.044715, 1.0, ALU.mult, ALU.add, [ks], [ks])
                TT_(sq[:, 0:n], sq[:, 0:n], usb[:, 0:n], ALU.mult, [ks, ku], [ks])
                return q

            def gelu_s2(q, dst, dstkey, n):
                usb, sq, th = usb2[q], sq2[q], th2[q]
                ku, ks, kt = "usb%d" % q, "sq%d" % q, "th%d" % q
                ACT(th[:, 0:n], sq[:, 0:n], AF.Tanh, [ks], [kt], scale=0.7978845608028654)
                STT(dst, th[:, 0:n], 1.0, usb[:, 0:n], ALU.add, ALU.mult, [kt, ku], [dstkey])

            loadW(0)
            it = 0
            load_xT(xTt[0], 0, PRE_T)
            for h in range(NH):
                ws = h % 2
                wk = ["W%d" % ws]
                if h + 1 < NH:
                    loadW(h + 1)
                for i in range(TT):
                    s = it % 2
                    nxt = it + 1
                    if nxt < NH * TT:
                        load_xT(xTt[nxt % 2], nxt % 2, PRE_T + nxt % TT)
                    xk = ["xTt%d" % s]
                    qs = []
                    for half in range(2):
                        b = half
                        for jj in range(2):
                            j = half * 2 + jj
                            for kc in range(KC):
                                MM(PS[b][:, jj * 256:(jj + 1) * 256], xTt[s][:, kc, 3 + j * 128:3 + (j + 1) * 128],
                                   W[ws][:, kc, 256:512], kc == 0, kc == KC - 1, wk + xk, ["ps%d" % b])
                        qs.append(gelu_s1(PS[b], "ps%d" % b, 512))
                    for half in range(2):
                        gelu_s2(qs[half], gv[:, half * 2:half * 2 + 2, :].rearrange("p j c -> p (j c)"), "gv%d" % half, 512)
                    gvk = ["gv0", "gv1"]
                    P.op("dve", lambda e: e.tensor_reduce(out=st4[:, 0:4], in_=gv, axis=AX.X, op=ALU.add), gvk, ["st_s1"])
                    ACT(sqv.rearrange("p j c -> p (j c)"), gv.rearrange("p j c -> p (j c)"), AF.Square, gvk, ["sqv"])
                    P.op("dve", lambda e: e.tensor_reduce(out=st4[:, 4:8], in_=sqv, axis=AX.X, op=ALU.add), ["sqv"], ["st_s2"])
                    TS(st4[:, 8:12], st4[:, 0:4], 1.0 / 256, None, ALU.mult, None, ["st_s1"], ["st_mean"])
                    TT_(st4[:, 12:16], st4[:, 8:12], st4[:, 8:12], ALU.mult, ["st_mean"], ["st_msq"])
                    STT(st4[:, 16:20], st4[:, 4:8], 1.0 / 256, st4[:, 12:16], ALU.mult, ALU.subtract, ["st_s2", "st_msq"], ["st_var"])
                    TS(st4[:, 16:20], st4[:, 16:20], 4.0 * EPS, None, ALU.add, None, ["st_var"], ["st_var"])
                    RSQRT(st4[:, 24:28], st4[:, 16:20], ["st_var"], ["st_rstd"])

                    def uz_block(cb_):
                        b = 2 + cb_
                        for kc in range(KC):
                            MM(PS[b], W[ws][:, kc, cb_ * 128:(cb_ + 1) * 128], xTt[s][:, kc, 3:515],
                               kc == 0, kc == KC - 1, wk + xk, ["ps%d" % b])
                        q_ = gelu_s1(PS[b], "ps%d" % b, 512)
                        b2 = 4 + cb_
                        for kc in range(KC):
                            MM(PS[b2], W[ws][:, kc, 512 + cb_ * 128:512 + (cb_ + 1) * 128], xTt[s][:, kc, 3:515],
                               kc == 0, kc == KC - 1, wk + xk, ["ps%d" % b2])
                        ACT(sz[cb_], PS[b2], AF.Silu, ["ps%d" % b2], ["sz%d" % cb_])
                        return q_

                    qu0 = uz_block(0)
                    STT(st4[:, 28:32], st4[:, 8:12], -1.0, st4[:, 24:28], ALU.mult, ALU.mult, ["st_mean", "st_rstd"], ["st_nmr"])
                    for j in range(4):
                        TS(gvn[:, j, :], gv[:, j, :], st4[:, 24 + j:25 + j], st4[:, 28 + j:29 + j], ALU.mult, ALU.add,
                           ["gv%d" % (j // 2), "st_rstd", "st_nmr"], ["gvn%d" % j])
                        TT_(gvn[:, j, :], gvn[:, j, :], sgw[ws], ALU.mult, ["gvn%d" % j, "sgw%d" % ws], ["gvn%d" % j])
                        TT_(vgn[:, j, :], gvn[:, j, :], sgb[ws], ALU.add, ["gvn%d" % j, "sgb%d" % ws], ["vgn%d" % j])
                    qu1 = uz_block(1)
                    gelu_s2(qu0, gu[0], "gu0", 512)
                    gelu_s2(qu1, gu[1], "gu1", 512)
                    for cb_ in range(2):
                        b = 6 + cb_
                        for j in range(4):
                            MM(PS[b][:, j * 128:(j + 1) * 128], vgn[:, j, cb_ * 128:(cb_ + 1) * 128], wsb[ws], True, True,
                               ["vgn%d" % j, "wsb%d" % ws], ["ps%d" % b])
                        STT(tmp, PS[b], 0.5, bsph[ws], ALU.mult, ALU.add, ["ps%d" % b, "bsph%d" % ws], ["tmp"])
                        TT_(tmp, tmp, gu[cb_], ALU.mult, ["tmp", "gu%d" % cb_], ["tmp"])
                        TT_(yT[cb_], tmp, sz[cb_], ALU.mult, ["tmp", "sz%d" % cb_], ["yT%d" % cb_])
                        fc = 16 + 2 * h + cb_
                        DMA("sp", yT_d[i * 4:(i + 1) * 4, :, fc, :].rearrange("c p t -> p c t"),
                            yT[cb_].rearrange("p (c t) -> p c t", t=128), ["yT%d" % cb_], [], "yTst%d" % cb_)
                    it += 1

        phase2()
        P.barrier()
        A.reset(pm2)

        def phase3():
            Wq = [A.bf16(KC * 128).rearrange("p (k c) -> p k c", c=128) for _ in range(2)]
            Woz = [A.bf16(KC * 512).rearrange("p (k c) -> p k c", c=512) for _ in range(2)]
            mnw = [A.f32(256) for _ in range(2)]
            xTt = [A.bf16(KC * 515).rearrange("p (k t) -> p k t", t=515) for _ in range(2)]
            pre = A.f32(515)
            acc = A.f32(512)
            qT = A.bf16(512)
            kT = [A.bf16(512) for _ in range(2)]
            ktok = [A.bf16(512).rearrange("p (j d) -> p j d", d=128) for _ in range(2)]
            vp = [A.bf16(4 * VW).rearrange("p (j v) -> p j v", v=VW) for _ in range(2)]
            ATm = A.bf16(512).rearrange("p (j t) -> p j t", t=128)
            to = [A.f32(256) for _ in range(2)]
            szz = [A.f32(256) for _ in range(2)]
            y2 = [[A.f32(256) for _ in range(4)] for _ in range(2)]
            y1 = A.f32(256)
            ybf = [A.bf16(256) for _ in range(4)]
            jk = A.bf16(256)
            stt = [A.f32(64) for _ in range(2)]
            yT = A.bf16(1024)
            CP(S.rearrange("p h v -> p (h v)"), Sinit.rearrange("p h v -> p (h v)"), ["Sinit"], ["S"])

            def loadW(h):
                s = h % 2
                DMA("pool", Wq[s], win_v[:, :, OFF_Q + h * 128:OFF_Q + (h + 1) * 128], [], ["Wq%d" % s], "Wq%d" % s)
                DMA("pool", Woz[s][:, :, 0:256], win_v[:, :, OFF_O + h * 256:OFF_O + (h + 1) * 256], [], ["Woz%d" % s], "Woz%d" % s)
                DMA("pool", Woz[s][:, :, 256:512], win_v[:, :, OFF_Z + h * 256:OFF_Z + (h + 1) * 256], [], ["Woz%d" % s], "Woz%d" % s)
                DMA("sp", mnw[s], mnw_d[h * 256:(h + 1) * 256].partition_broadcast(128), [], ["mnw%d" % s], "mnw%d" % s)

            def load_tile(itn):
                h_, i_ = divmod(itn, TT)
                s_ = itn % 2
                load_xT(xTt[s_], s_, PRE_T + i_)
                DMA("sp", kT[s_], kT_d[h_, :, i_ * 512:(i_ + 1) * 512], [], ["kT%d" % s_], "kTl%d" % s_)
                DMA("sp", ktok[s_], kt_d[h_, i_ * 4:(i_ + 1) * 4, :, :].rearrange("c p d -> p c d"), [], ["ktok%d" % s_], "ktl%d" % s_)
                DMA("sp", vp[s_], v_d[h_, i_ * 4:(i_ + 1) * 4, :, :].rearrange("c p v -> p c v"), [], ["vp%d" % s_], "vl%d" % s_)

            def Q_mm(h, i, s, ws):
                fpc_mm(Wq[ws], xTt[s], s, pre, 0, PS[3][:, 0:3], "ps3", ["Wq%d" % ws])

            def QA_part(h, i, s, ws):
                fpc_conv(h, pre, acc, qT, "qT")
                for j in range(4):
                    tsl = slice(j * 128, (j + 1) * 128)
                    MM(PS[1][:, tsl], kT[s][:, tsl], qT[:, tsl], True, True, ["kT%d" % s, "qT"], ["ps1"])
                TT_(ATm, PS[1].rearrange("p (j t) -> p j t", t=128), trif.unsqueeze(1).to_broadcast([128, 4, 128]), ALU.mult,
                    ["ps1", "trif"], ["ATm"])

            def C_oz(h, i, s, ws, j):
                bo = (2, 0)[j % 2]
                for kc in range(KC):
                    MM(PS[bo], xTt[s][:, kc, 3 + j * 128:3 + (j + 1) * 128], Woz[ws][:, kc, :],
                       kc == 0, kc == KC - 1, ["xTt%d" % s, "Woz%d" % ws], ["ps%d" % bo])

            def C_part(h, i, s, ws):
                p = s
                st = stt[p]
                for j in range(4):
                    c = PRE_C + i * 4 + j
                    sj = j % 2
                    tsl = slice(j * 128, (j + 1) * 128)
                    bo = (2, 0)[j % 2]
                    if j > 0:
                        C_oz(h, i, s, ws, j)
                    ACT(to[sj], PS[bo][:, 0:256], AF.Tanh, ["ps%d" % bo], ["to%d" % sj], scale=0.5)
                    ACT(szz[sj], PS[bo][:, 256:512], AF.Silu, ["ps%d" % bo], ["szz%d" % sj])
                    STT(y2[p][j], to[sj], 1.0, szz[sj], ALU.add, ALU.mult, ["to%d" % sj, "szz%d" % sj], ["y2_%d_%d" % (p, j)])
                    first = (i == 0 and j == 0)
                    aprev = 1.0 if first else v3(aL)[:, c - 1, h:h + 1]
                    if first:
                        CP(Sbf[:, 0, 0:257], S[:, h, 0:257], ["S"], ["Sbf0"])
                    bn = 4 + j
                    MM(PS[bn][:, 0:257], ATm[:, j, :], vp[s][:, j, 0:257], True, False, ["ATm", "vp%d" % s], ["ps%d" % bn])
                    MM(PS[bn][:, 0:257], qT[:, tsl], Sbf[:, sj, 0:257], False, True, ["qT", "Sbf%d" % sj], ["ps%d" % bn])
                    MM(PS[3][:, 0:257], ktok[s][:, j, :], vp[s][:, j, 0:257], True, True, ["ktok%d" % s, "vp%d" % s], ["ps3"])
                    STT(S[:, h, 0:257], S[:, h, 0:257], aprev, PS[3][:, 0:257], ALU.mult, ALU.add,
                        ["ps3", "aL", "S"], ["S"])
                    if j < 3 or i < TT - 1:
                        TS(Sbf[:, (j + 1) % 2, 0:257], S[:, h, 0:257], v3(aL)[:, c, h:h + 1], None, ALU.mult, None,
                           ["S", "aL"], ["Sbf%d" % ((j + 1) % 2)])
                    ACT(st[:, 32 + j:33 + j], PS[bn][:, 256:257], AF.Abs, ["ps%d" % bn, "ee"], ["st_ab%d_%d" % (p, j)],
                        scale=v3(ee)[:, c, h:h + 1])
                    TS(st[:, j:j + 1], st[:, 32 + j:33 + j], 1.0, None, ALU.max, None, ["st_ab%d_%d" % (p, j)], ["st_dd%d_%d" % (p, j)])
                    RECIP(st[:, 4 + j:5 + j], st[:, j:j + 1], ["st_dd%d_%d" % (p, j)], ["st_r%d_%d" % (p, j)])
                    TT_(st[:, 8 + j:9 + j], st[:, 4 + j:5 + j], v3(ee)[:, c, h:h + 1], ALU.mult, ["st_r%d_%d" % (p, j), "ee"],
                        ["st_r2_%d_%d" % (p, j)])
                    ACT(jk, PS[bn][:, 0:256], AF.Square, ["ps%d" % bn, "st_r2_%d_%d" % (p, j)], ["st_ssq%d_%d" % (p, j)],
                        scale=st[:, 8 + j:9 + j], accum_out=st[:, 12 + j:13 + j])

                ssqk = ["st_ssq%d_%d" % (p, j) for j in range(4)]
                TS(st[:, 16:20], st[:, 12:16], 1.0 / 256, EPS, ALU.mult, ALU.add, ssqk, ["st_v%d" % p])
                RSQRT(st[:, 24:28], st[:, 16:20], ["st_v%d" % p], ["st_rs%d" % p])
                TT_(st[:, 28:32], st[:, 24:28], st[:, 8:12], ALU.mult, ["st_rs%d" % p] + ["st_r2_%d_%d" % (p, j) for j in range(4)],
                    ["st_sc%d" % p], eng="pool")

            def E_dve(h, i, s, ws):
                p = s
                st = stt[p]
                for j in range(4):
                    bn = 4 + j
                    yb = ybf[j]
                    STT(y1, PS[bn][:, 0:256], st[:, 28 + j:29 + j], mnw[ws], ALU.mult, ALU.mult,
                        ["ps%d" % bn, "st_sc%d" % p, "mnw%d" % ws], ["y1"])
                    STT(yb, y1, 0.5, y2[p][j], ALU.mult, ALU.mult, ["y1", "y2_%d_%d" % (p, j)], ["ybf%d" % j])

            def E_part(h, i, s, ws):
                for j in range(4):
                    yb = ybf[j]
                    for vb in range(2):
                        r = vb * 4 + j
                        P.op("pe", lambda e, o=PSB[0][:, r * 128:(r + 1) * 128], a=yb[:, vb * 128:(vb + 1) * 128]:
                             e.transpose(o, a, ident), ["ybf%d" % j, "ident"], ["ps0"])
                CP(yT, PSB[0], ["ps0"], ["yT"], eng="act")
                for vb in range(2):
                    fc = 2 * h + vb
                    DMA("sp", yT_d[i * 4:(i + 1) * 4, :, fc, :].rearrange("c p t -> p c t"),
                        yT[:, vb * 512:(vb + 1) * 512].rearrange("p (c t) -> p c t", t=128), ["yT"], [], "yTst3_%d" % vb)

            loadW(0)
            load_tile(0)
            it = 0
            prev = None
            for h in range(NH):
                ws = h % 2
                for i in range(TT):
                    s = it % 2
                    if it + 1 < NH * TT:
                        load_tile(it + 1)
                    if prev is not None:
                        E_dve(*prev)
                    Q_mm(h, i, s, ws)
                    if prev is not None:
                        E_part(*prev)
                    if i == 0 and h + 1 < NH:
                        loadW(h + 1)
                    C_oz(h, i, s, ws, 0)
                    QA_part(h, i, s, ws)
                    C_part(h, i, s, ws)
                    prev = (h, i, s, ws)
                    it += 1
            E_dve(*prev)
            E_part(*prev)

        phase3()
        P.barrier()
        A.reset(pm_p4)

        def phase4():
            wo = A.bf16(32 * D).rearrange("p (f c) -> p f c", c=D)
            fnw = A.f32(D)
            yTt = [A.bf16(32 * 128).rearrange("p (f t) -> p f t", t=128) for _ in range(2)]
            xr = [A.f32(D) for _ in range(2)]
            junk4 = A.bf16(D)
            s4 = A.f32(8)
            for f0 in range(0, 32, 4):
                DMA("pool", wo[:, f0:f0 + 4, :], wout_v[:, f0:f0 + 4, :], [], ["wo"], "wo")
            DMA("sp", fnw, fnw_d.partition_broadcast(128), [], ["fnw"], "fnw")

            def load(c):
                s_ = c % 2
                DMA("sp", yTt[s_], yT_d[c], [], ["yTt%d" % s_], "yTt%d" % s_)
                DMA("sp", xr[s_], x_d[c * 128:(c + 1) * 128, :], [], ["xr%d" % s_], "xr%d" % s_)

            load(0)
            for c in range(NCH):
                s = c % 2
                if c + 1 < NCH:
                    load(c + 1)
                bb = 4 * (c % 2)
                for fc in range(32):
                    for n in range(4):
                        MM(PS[bb + n], yTt[s][:, fc, :], wo[:, fc, n * 512:(n + 1) * 512], fc == 0, fc == 31,
                           ["yTt%d" % s, "wo"], ["ps%d" % (bb + n)])
                for n in range(4):
                    TT_(xr[s][:, n * 512:(n + 1) * 512], PS[bb + n], xr[s][:, n * 512:(n + 1) * 512], ALU.add,
                        ["ps%d" % (bb + n), "xr%d" % s], ["xr%d" % s])
                ACT(junk4, xr[s], AF.Square, ["xr%d" % s], ["junk4", "s4a"], accum_out=s4[:, 0:1])
                TS(s4[:, 1:2], s4[:, 0:1], 1.0 / D, EPS, ALU.mult, ALU.add, ["s4a"], ["s4b"])
                RSQRT(s4[:, 3:4], s4[:, 1:2], ["s4b"], ["s4d"])
                STT(xr[s], xr[s], s4[:, 3:4], fnw, ALU.mult, ALU.mult, ["xr%d" % s, "s4d", "fnw"], ["xr%d" % s])
                DMA("sp", out_d[c * 128:(c + 1) * 128, :], xr[s], ["xr%d" % s], ["out%d" % s], "ost%d" % s)

        phase4()
        P.final_wait("sp", ["out0", "out1"])
        P.emit()
    return nc


_NC_CACHE = {}


def kernel(x, norm_w, w_in, conv_w, conv_b, b_igate, b_fgate, mlstm_norm_w,
           sgu_norm_w, sgu_norm_b, w_spatial, b_spatial, w_out, final_norm_w):
    f = lambda a: np.ascontiguousarray(np.asarray(a, dtype=np.float32))
    x = f(x)
    B, S_, _ = x.shape
    assert B * NSEG == NCORES and S_ == NSEG * NTOK
    if "nc" not in _NC_CACHE:
        _NC_CACHE["nc"] = build_program()
    nc = _NC_CACHE["nc"]
    conv_wT = f(np.asarray(conv_w).reshape(4, 16, 128).transpose(2, 1, 0).reshape(128, 64))
    conv_bT = f(np.asarray(conv_b).reshape(16, 128).T)
    b_if = f(np.concatenate([np.asarray(b_igate), np.asarray(b_fgate)]))
    w_spT = f(np.asarray(w_spatial).transpose(0, 2, 1))
    ident = np.eye(128, dtype=np.float32)
    tri = np.triu(np.ones((128, 128), dtype=np.float32))
    ones = np.ones((128, 128), dtype=np.float32)
    shared = {
        "norm_w": f(norm_w), "w_in": f(w_in), "conv_wT": conv_wT, "conv_bT": conv_bT, "b_if": b_if,
        "mlstm_norm_w": f(mlstm_norm_w), "sgu_norm_w": f(sgu_norm_w), "sgu_norm_b": f(sgu_norm_b),
        "w_spT": w_spT, "b_spatial": f(b_spatial), "w_out": f(w_out), "final_norm_w": f(final_norm_w),
        "ident": ident, "tri": tri, "ones": ones,
    }
    in_maps = []
    for c in range(NCORES):
        b, sg = divmod(c, NSEG)
        t0 = sg * NTOK
        xs = x[b, t0:t0 + NTOK]
        xh = x[b, t0 - 128:t0] if sg > 0 else np.zeros((128, D), np.float32)
        m = dict(shared)
        m["x"] = np.ascontiguousarray(xs)
        m["xh"] = np.ascontiguousarray(xh)
        xe = np.zeros((PRE_C * 128, D), np.float32)
        if t0 > 0:
            xe[PRE_C * 128 - t0:] = x[b, 0:t0]
        mch = np.zeros(NCHX, np.float32)
        mch[PRE_C - t0 // 128:] = 1.0
        m["xe"] = xe
        m["mch"] = mch
        m["xh"] = np.zeros((128, D), np.float32)
        in_maps.append(m)
    res = run_bass_kernel_spmd(nc, in_maps, core_ids=list(range(NCORES)))
    out = np.empty((B, S_, D), dtype=np.float32)
    for c in range(NCORES):
        b, sg = divmod(c, NSEG)
        out[b, sg * NTOK:(sg + 1) * NTOK] = res.results[c]["out"]
    return out
```
